# Optimizing a Trainium2 kernel written in Bass

```python
import math
import jax, jax.numpy as jnp
from jax import lax
import numpy as np

D_MODEL = 1024
BATCH = 16
SEQ = 2048
DEPTH = 2
DEC_BATCH = 8
DEC_SEQ = 16
PAST_LEN = 2048

CHUNK = 64
Q_BLOCK = 128
M_EXPAND = 2
D_INNER = M_EXPAND * D_MODEL
M_HEADDIM = 64
M_HEADS = D_INNER // M_HEADDIM
M_GROUPS = 8
M_HPG = M_HEADS // M_GROUPS
D_STATE = 128
CONV_K = 4
CONV_DIM = D_INNER + 2 * M_GROUPS * D_STATE
SSD_CHUNK = CHUNK
A_HEADS = 16
A_HEADDIM = 64
D_ATTN = A_HEADS * A_HEADDIM
IN_SIZES = (D_INNER, CONV_DIM, M_HEADS, D_ATTN, D_ATTN, D_ATTN, A_HEADS, D_ATTN, D_MODEL, D_MODEL)
D_IN = D_INNER + CONV_DIM + M_HEADS + 4 * D_ATTN + A_HEADS + 2 * D_MODEL
DN_ALPHA = (2 * DEPTH) ** 0.25
DN_BETA = (8 * DEPTH) ** -0.25
LN_EPS = 1e-5
RMS_EPS = 1e-5

kernel_name = 'hybrid_ssd_fox_stream_step'


def split_projection(h):
    idx = [int(i) for i in np.cumsum(IN_SIZES)[:-1]]
    return jnp.split(h, idx, axis=-1)


def layer_norm(x, g, b):
    xf = x.astype(jnp.float32)
    mu = jnp.mean(xf, axis=-1, keepdims=True)
    var = jnp.mean(jnp.square(xf - mu), axis=-1, keepdims=True)
    y = (xf - mu) * lax.rsqrt(var + LN_EPS) * g.astype(jnp.float32) + b.astype(jnp.float32)
    return y.astype(x.dtype)


def causal_conv(xbc, conv_state, w, bias):
    L = xbc.shape[1]
    xpad = jnp.concatenate([conv_state.astype(xbc.dtype), xbc], axis=1)
    out = bias
    for j in range(CONV_K):
        out = out + xpad[:, j:j + L] * w[j]
    return jax.nn.silu(out), xpad[:, xpad.shape[1] - (CONV_K - 1):]


def ssd_scan(xs, dt, A, Bm, Cm, s0, chunk):
    b, L = xs.shape[:2]
    nc = L // chunk
    xc = xs.reshape(b, nc, chunk, M_GROUPS, M_HPG, M_HEADDIM)
    dtc = dt.reshape(b, nc, chunk, M_GROUPS, M_HPG)
    Bc = Bm.reshape(b, nc, chunk, M_GROUPS, D_STATE)
    Cc = Cm.reshape(b, nc, chunk, M_GROUPS, D_STATE)
    acum = jnp.cumsum(dtc * A, axis=2)
    seg = acum[:, :, :, None] - acum[:, :, None]
    causal = jnp.tril(jnp.ones((chunk, chunk), dtype=bool))[:, :, None, None]
    decay = jnp.exp(jnp.where(causal, seg, -jnp.inf))
    cb = jnp.einsum('bclgn,bcsgn->bclsg', Cc, Bc)
    m = cb[..., None] * decay * dtc[:, :, None]
    y_diag = jnp.einsum('bclsgr,bcsgrp->bclgrp', m, xc)

    def step(s_in, inp):
        x_c, dt_c, b_c, c_c, acum_c = inp
        y_off = jnp.einsum('blgn,bgrpn->blgrp', c_c, s_in) * jnp.exp(acum_c)[..., None]
        a_end = acum_c[:, -1]
        w_end = jnp.exp(a_end[:, None] - acum_c) * dt_c
        s_c = jnp.einsum('blgn,blgr,blgrp->bgrpn', b_c, w_end, x_c)
        s_out = jnp.exp(a_end)[..., None, None] * s_in + s_c
        return s_out, y_off

    mv = lambda a: jnp.moveaxis(a, 1, 0)
    s_fin, y_off = lax.scan(step, s0, (mv(xc), mv(dtc), mv(Bc), mv(Cc), mv(acum)))
    y = y_diag + jnp.moveaxis(y_off, 0, 1)
    return y.reshape(b, L, M_GROUPS, M_HPG, M_HEADDIM), s_fin


def fox_block(q, k, v, cq, ck, q_pos):
    s = jnp.einsum('bqhd,bkhd->bhqk', q, k).astype(jnp.float32) * (A_HEADDIM ** -0.5)
    bias = jnp.moveaxis(cq, 1, 2)[..., :, None] - jnp.moveaxis(ck, 1, 2)[..., None, :]
    mask = jnp.arange(k.shape[1])[None, :] <= q_pos[:, None]
    p = jax.nn.softmax(jnp.where(mask, s + bias, -jnp.inf), axis=-1)
    return jnp.einsum('bhqk,bkhd->bqhd', p.astype(v.dtype), v)


def fox_prompt(q, k, v, logf):
    b, L = q.shape[:2]
    nb = L // Q_BLOCK
    c = jnp.cumsum(logf.astype(jnp.float32), axis=1)
    qb = jnp.moveaxis(q.reshape(b, nb, Q_BLOCK, A_HEADS, A_HEADDIM), 1, 0)
    cb = jnp.moveaxis(c.reshape(b, nb, Q_BLOCK, A_HEADS), 1, 0)
    pos = jnp.arange(L).reshape(nb, Q_BLOCK)
    out = lax.map(lambda a: fox_block(a[0], k, v, a[1], c, a[2]), (qb, cb, pos))
    return jnp.moveaxis(out, 0, 1).reshape(b, L, A_HEADS, A_HEADDIM)


def fox_sample(q, k, v, logf, cache_k, cache_v, cache_logf):
    past = cache_k.shape[1]
    k_all = jnp.concatenate([cache_k.astype(k.dtype), k], axis=1)
    v_all = jnp.concatenate([cache_v.astype(v.dtype), v], axis=1)
    c_all = jnp.cumsum(jnp.concatenate([cache_logf.astype(jnp.float32), logf.astype(jnp.float32)], axis=1), axis=1)
    q_pos = past + jnp.arange(q.shape[1])
    return fox_block(q, k_all, v_all, c_all[:, past:], c_all, q_pos)


def mixer_layer(x, conv_state, ssm_state, past_kvf, chunk, w_in, conv_w, conv_b, dt_bias, a_log,
                d_skip, mnorm_w, b_f, b_merge, w_br_m, w_br_a, w_out, ln_g, ln_b):
    b, L = x.shape[:2]
    f32 = jnp.float32
    h = jnp.einsum('bld,de->ble', x, w_in)
    z, xbc, dt_raw, q, k, v, f_raw, g_path, mg_m, mg_a = split_projection(h)

    xbc, new_conv = causal_conv(xbc, conv_state, conv_w, conv_b)
    xs, Bm, Cm = jnp.split(xbc, [D_INNER, D_INNER + M_GROUPS * D_STATE], axis=-1)
    xs = xs.reshape(b, L, M_GROUPS, M_HPG, M_HEADDIM).astype(f32)
    Bm = Bm.reshape(b, L, M_GROUPS, D_STATE).astype(f32)
    Cm = Cm.reshape(b, L, M_GROUPS, D_STATE).astype(f32)
    dt = jax.nn.softplus(dt_raw.astype(f32) + dt_bias.astype(f32)).reshape(b, L, M_GROUPS, M_HPG)
    A = -jnp.exp(a_log.astype(f32)).reshape(M_GROUPS, M_HPG)
    s0 = ssm_state.astype(f32).reshape(b, M_GROUPS, M_HPG, M_HEADDIM, D_STATE)
    y, new_ssm = ssd_scan(xs, dt, A, Bm, Cm, s0, chunk)
    y = y + d_skip.astype(f32).reshape(M_GROUPS, M_HPG)[:, :, None] * xs
    y = y.reshape(b, L, D_INNER) * jax.nn.silu(z.astype(f32))
    yg = y.reshape(b, L, M_GROUPS, D_INNER // M_GROUPS)
    yg = yg * lax.rsqrt(jnp.mean(jnp.square(yg), axis=-1, keepdims=True) + RMS_EPS)
    y_m = (yg.reshape(b, L, D_INNER) * mnorm_w.astype(f32)).astype(x.dtype)

    q = q.reshape(b, L, A_HEADS, A_HEADDIM)
    k = k.reshape(b, L, A_HEADS, A_HEADDIM)
    v = v.reshape(b, L, A_HEADS, A_HEADDIM)
    logf = jax.nn.log_sigmoid((f_raw + b_f).astype(f32))
    if past_kvf is None:
        o = fox_prompt(q, k, v, logf)
    else:
        o = fox_sample(q, k, v, logf, past_kvf[0], past_kvf[1], past_kvf[2])
    y_a = o.reshape(b, L, D_ATTN) * jax.nn.silu(g_path)

    gate_m = jax.nn.sigmoid(mg_m + b_merge[:D_MODEL])
    gate_a = jax.nn.sigmoid(mg_a + b_merge[D_MODEL:])
    u = gate_m * jnp.einsum('ble,ed->bld', y_m, w_br_m) + gate_a * jnp.einsum('ble,ed->bld', y_a, w_br_a)
    out = jnp.einsum('bld,de->ble', u, w_out)
    x_new = layer_norm(DN_ALPHA * x + out, ln_g, ln_b)
    return x_new, (k, v, logf, new_ssm.reshape(b, M_HEADS, M_HEADDIM, D_STATE), new_conv)


def setup_inputs(seed: int = 0) -> dict:
    key = jax.random.key(seed)
    ks = jax.random.split(key, 24)
    nrm = jax.random.normal
    col_scale = jnp.concatenate([
        jnp.ones((D_INNER + CONV_DIM + M_HEADS + 2 * D_ATTN,), jnp.float32),
        jnp.full((D_ATTN,), DN_BETA, jnp.float32),
        jnp.ones((A_HEADS + D_ATTN + 2 * D_MODEL,), jnp.float32)])
    w_in = nrm(ks[7], (DEPTH, D_MODEL, D_IN), jnp.float32) * (D_MODEL ** -0.5) * col_scale
    u = jax.random.uniform(ks[10], (DEPTH, M_HEADS), jnp.float32)
    dt0 = jnp.exp(u * (math.log(0.1) - math.log(1e-3)) + math.log(1e-3))
    dt_bias = dt0 + jnp.log(-jnp.expm1(-dt0))
    return {
        'x_prompt': nrm(ks[0], (BATCH, SEQ, D_MODEL), jnp.float32),
        'x_sample': nrm(ks[1], (DEC_BATCH, DEC_SEQ, D_MODEL), jnp.float32),
        'cache_k': nrm(ks[2], (DEPTH, DEC_BATCH, PAST_LEN, A_HEADS, A_HEADDIM), jnp.float32),
        'cache_v': DN_BETA * nrm(ks[3], (DEPTH, DEC_BATCH, PAST_LEN, A_HEADS, A_HEADDIM), jnp.float32),
        'cache_logf': jax.nn.log_sigmoid(1.0 + nrm(ks[4], (DEPTH, DEC_BATCH, PAST_LEN, A_HEADS), jnp.float32)),
        'state_ssm': 0.1 * nrm(ks[5], (DEPTH, DEC_BATCH, M_HEADS, M_HEADDIM, D_STATE), jnp.float32),
        'state_conv': nrm(ks[6], (DEPTH, DEC_BATCH, CONV_K - 1, CONV_DIM), jnp.float32),
        'w_in': w_in,
        'conv_w': nrm(ks[8], (DEPTH, CONV_K, CONV_DIM), jnp.float32) * (CONV_K ** -0.5),
        'conv_b': 0.01 * nrm(ks[9], (DEPTH, CONV_DIM), jnp.float32),
        'dt_bias': dt_bias,
        'a_log': jnp.log(jax.random.uniform(ks[11], (DEPTH, M_HEADS), jnp.float32, 1.0, 16.0)),
        'd_skip': 1.0 + 0.1 * nrm(ks[12], (DEPTH, M_HEADS), jnp.float32),
        'mnorm_w': 1.0 + 0.1 * nrm(ks[13], (DEPTH, D_INNER), jnp.float32),
        'b_f': 1.0 + 0.5 * nrm(ks[14], (DEPTH, A_HEADS), jnp.float32),
        'b_merge': 0.1 * nrm(ks[15], (DEPTH, 2 * D_MODEL), jnp.float32),
        'w_br_m': nrm(ks[16], (DEPTH, D_INNER, D_MODEL), jnp.float32) * (D_INNER ** -0.5) * DN_BETA,
        'w_br_a': nrm(ks[17], (DEPTH, D_ATTN, D_MODEL), jnp.float32) * (D_ATTN ** -0.5) * DN_BETA,
        'w_out': nrm(ks[18], (DEPTH, D_MODEL, D_MODEL), jnp.float32) * (D_MODEL ** -0.5) * DN_BETA,
        'ln_g': 1.0 + 0.1 * nrm(ks[19], (DEPTH, D_MODEL), jnp.float32),
        'ln_b': 0.01 * nrm(ks[20], (DEPTH, D_MODEL), jnp.float32),
    }


def reference(x_prompt, x_sample, cache_k, cache_v, cache_logf, state_ssm, state_conv, w_in, conv_w,
              conv_b, dt_bias, a_log, d_skip, mnorm_w, b_f, b_merge, w_br_m, w_br_a, w_out, ln_g, ln_b):
    xp = x_prompt
    xs = x_sample
    bp = xp.shape[0]
    pk, pv, pf, pssm, pconv = [], [], [], [], []
    sk, sv, sf, sssm, sconv = [], [], [], [], []
    for l in range(DEPTH):
        lw = (w_in[l], conv_w[l], conv_b[l], dt_bias[l], a_log[l], d_skip[l], mnorm_w[l], b_f[l],
              b_merge[l], w_br_m[l], w_br_a[l], w_out[l], ln_g[l], ln_b[l])
        conv0 = jnp.zeros((bp, CONV_K - 1, CONV_DIM), xp.dtype)
        ssm0 = jnp.zeros((bp, M_HEADS, M_HEADDIM, D_STATE), jnp.float32)
        xp, st_p = mixer_layer(xp, conv0, ssm0, None, SSD_CHUNK, *lw)
        xs, st_s = mixer_layer(xs, state_conv[l], state_ssm[l], (cache_k[l], cache_v[l], cache_logf[l]),
                               xs.shape[1], *lw)
        pk.append(st_p[0]); pv.append(st_p[1]); pf.append(st_p[2]); pssm.append(st_p[3]); pconv.append(st_p[4])
        sk.append(st_s[0]); sv.append(st_s[1]); sf.append(st_s[2]); sssm.append(st_s[3]); sconv.append(st_s[4])
    new_k_p = jnp.stack(pk)
    new_v_p = jnp.stack(pv)
    new_logf_p = jnp.stack(pf)
    new_ssm_p = jnp.stack(pssm)
    new_conv_p = jnp.stack(pconv)
    new_k_s = jnp.stack(sk)
    new_v_s = jnp.stack(sv)
    new_logf_s = jnp.stack(sf)
    new_ssm_s = jnp.stack(sssm)
    new_conv_s = jnp.stack(sconv)
    return (xp, xs, new_k_p, new_v_p, new_logf_p, new_ssm_p, new_conv_p,
            new_k_s, new_v_s, new_logf_s, new_ssm_s, new_conv_s)
```

```python
import numpy as np
import concourse.bass as bass
import concourse.mybir as mybir
from concourse.bass_utils import run_bass_kernel_spmd

F32 = mybir.dt.float32
BF16 = mybir.dt.bfloat16
AF = mybir.ActivationFunctionType
ALU = mybir.AluOpType

ENGS = ("pe", "act", "dve", "pool", "sp")

D = 1024
DIN = 12336
NH_M = 32
NH_A = 16
C_Z, C_X, C_B, C_C, C_DT, C_Q, C_K, C_V, C_F, C_G, C_MM, C_MA = 0, 2048, 4096, 5120, 6144, 6176, 7200, 8224, 9248, 9264, 10288, 11312
DN_ALPHA = (2 * 2) ** 0.25
NPB_L = 2160
NPP_L = 192


class Buf:
    __slots__ = ("name", "w", "r", "dw", "dr", "wsem", "rsem", "wcnt", "rcnt", "psum")

    def __init__(self, name):
        self.name = name
        self.psum = False
        self.w = {}
        self.r = {}
        self.dw = {}
        self.dr = {}
        self.wsem = {}
        self.rsem = {}
        self.wcnt = {}
        self.rcnt = {}


class DBuf:
    def __init__(self, name):
        self.name = name
        self.w = {}


class Tile:
    def __init__(self, t, buf):
        self.t = t
        self.buf = buf

    def __getitem__(self, k):
        return self.t[k]


class Rot:
    def __init__(self, items):
        self.items = items
        self.i = 0

    def next(self):
        x = self.items[self.i % len(self.items)]
        self.i += 1
        return x


class Sched:
    def __init__(self, nc):
        self.nc = nc
        self.ops = {e: [] for e in ENGS}
        self.out_waits = []
        self.nsem = 0
        self.phase = "pro"

    def _sem(self, name):
        self.nsem += 1
        return self.nc.alloc_semaphore(name=f"s{self.nsem}_{name}")

    def sbuf(self, name, shape, dtype):
        return Tile(self.nc.alloc_sbuf_tensor(name, list(shape), dtype), Buf(name))

    def psum(self, name, shape, dtype=F32):
        t = Tile(self.nc.alloc_psum_tensor(name, list(shape), dtype), Buf(name))
        t.buf.psum = True
        return t

    @staticmethod
    def _bufs(xs):
        out = []
        for x in xs:
            if x is None:
                continue
            out.append(x.buf if isinstance(x, Tile) else x)
        return out

    def op(self, eng, fn, reads=(), writes=()):
        reads = self._bufs(reads)
        writes = self._bufs(writes)
        deps = []
        for b in reads:
            for e, i in b.w.items():
                if e == eng and eng == "pe":
                    continue
                deps.append(("e", e, i))
            for sc in b.dw.values():
                deps.append(("d",) + sc)
            if b.psum:
                for e, i in b.r.items():
                    if e != eng:
                        deps.append(("e", e, i))
        for b in writes:
            for e, i in b.w.items():
                if e != eng or eng != "pe":
                    deps.append(("e", e, i))
            for e, i in b.r.items():
                if e != eng or eng != "pe":
                    deps.append(("e", e, i))
            for sc in b.dw.values():
                deps.append(("d",) + sc)
            for sc in b.dr.values():
                deps.append(("d",) + sc)
        idx = len(self.ops[eng])
        self.ops[eng].append({"fn": fn, "deps": deps, "signal": False, "dma": None, "phase": self.phase})
        for b in reads:
            b.r[eng] = idx
        for b in writes:
            b.w[eng] = idx
        return idx

    def dma(self, q, out_ap, in_ap, reads=(), writes=(), final=False, dram_r=None, dram_w=None, serialize=False, **kw):
        reads = self._bufs(reads)
        writes = self._bufs(writes)
        assert len(reads) + len(writes) == 1
        deps = []
        sw = (q == "pool")
        for b in reads:
            for e, i in b.w.items():
                deps.append(("e", e, i))
            for sc in b.dw.values():
                deps.append(("d",) + sc)
        for b in writes:
            for e, i in b.w.items():
                deps.append(("e", e, i))
            for e, i in b.r.items():
                deps.append(("e", e, i))
            for sc in b.dr.values():
                deps.append(("d",) + sc)
            if serialize:
                for sc in b.dw.values():
                    deps.append(("d",) + sc)
        if dram_r is not None:
            for s_, c_ in dram_r.w.values():
                deps.append(("d", s_, c_))
        sem = None
        for b in writes:
            if sw not in b.wsem:
                b.wsem[sw] = self._sem(("ws" if sw else "wh") + b.name)
                b.wcnt[sw] = 0
            b.wcnt[sw] += 16
            sem, cnt = b.wsem[sw], b.wcnt[sw]
            b.dw[id(sem)] = (sem, cnt)
        for b in reads:
            if sw not in b.rsem:
                b.rsem[sw] = self._sem(("rs" if sw else "rh") + b.name)
                b.rcnt[sw] = 0
            b.rcnt[sw] += 16
            sem, cnt = b.rsem[sw], b.rcnt[sw]
            b.dr[id(sem)] = (sem, cnt)
            if final:
                self.out_waits.append(b)
        if dram_w is not None:
            dram_w.w[id(sem)] = (sem, cnt)

        def fn(eng, out_ap=out_ap, in_ap=in_ap, kw=kw):
            return eng.dma_start(out=out_ap, in_=in_ap, **kw)
        self.ops[q].append({"fn": fn, "deps": deps, "signal": False, "dma": sem})

    def emit(self):
        nc = self.nc
        for e in ENGS:
            for o in self.ops[e]:
                for d in o["deps"]:
                    if d[0] == "e":
                        self.ops[d[1]][d[2]]["signal"] = True
        esem = {}
        for e in ENGS:
            c = 0
            for o in self.ops[e]:
                if o["signal"]:
                    assert o["dma"] is None
                    c += 1
                    o["sigval"] = c
            if c:
                esem[e] = self._sem("eng_" + e)
        out_waits = self.out_waits

        def body(e):
            def f(eng):
                known = {}
                for o in self.ops[e]:
                    need = {}
                    for d in o["deps"]:
                        if d[0] == "e":
                            s = esem[d[1]]
                            v = self.ops[d[1]][d[2]]["sigval"]
                        else:
                            s, v = d[1], d[2]
                        k = id(s)
                        if known.get(k, 0) >= v:
                            continue
                        if k not in need or need[k][1] < v:
                            need[k] = (s, v)
                    for k, (s, v) in need.items():
                        eng.wait_ge(s, v)
                        known[k] = v
                    inst = o["fn"](eng)
                    if o["dma"] is not None:
                        inst.then_inc(o["dma"], 16)
                    elif o["signal"]:
                        inst.then_inc(esem[e], 1)
                if e == "pool":
                    seen = set()
                    for b in out_waits:
                        if id(b) in seen:
                            continue
                        seen.add(id(b))
                        for sem_, cnt_ in b.dr.values():
                            eng.wait_ge(sem_, cnt_)
            return f

        with nc.Block() as block:
            block.tensor(body("pe"))
            block.scalar(body("act"))
            block.vector(body("dve"))
            block.gpsimd(body("pool"))
            block.sync(body("sp"))


def build(SEQ, PAST, NB, DEC=16):
    NCHP = SEQ // 128
    NCHS = PAST // 128
    NCH = max(NCHP, NCHS + 1)
    nc = bass.Bass("TRN2", target_bir_lowering=False)

    def din(name, shape, dt=F32):
        return nc.dram_tensor(name, list(shape), dt, kind="ExternalInput").ap()

    def dout(name, shape, dt=F32):
        return nc.dram_tensor(name, list(shape), dt, kind="ExternalOutput").ap()

    def dint(name, shape, dt=BF16):
        return nc.dram_tensor(name, list(shape), dt, kind="Internal").ap()

    xp = din("xp", [NB, SEQ, D])
    xs = din("xs", [DEC, D])
    ck = din("ck", [2, PAST, D])
    cv = din("cv", [2, PAST, D])
    clf = din("clf", [2, PAST, NH_A])
    sssm = din("sssm", [2, 2048, 128])
    sconv = din("sconv", [128, 2, 32, 3])
    w_in = din("w_in", [2, D, DIN])
    w_br_m = din("w_br_m", [2, 2048, D])
    w_br_a = din("w_br_a", [2, D, D])
    w_out = din("w_out", [2, D, D])
    pbc_d = din("pbc", [128, 2 * NPB_L])
    ppp_d = din("ppp", [128, 2 * NPP_L])

    y_p = dout("y_p", [NB, SEQ, D])
    y_s = dout("y_s", [DEC, D])
    nk_p = dout("nk_p", [2, NB, SEQ, D])
    nv_p = dout("nv_p", [2, NB, SEQ, D])
    nlf_p = dout("nlf_p", [2, NB, SEQ, NH_A])
    nssm_p = dout("nssm_p", [2, NB, 2048, 128])
    nconv_p = dout("nconv_p", [2, NB, 128, 32, 3])
    nk_s = dout("nk_s", [2, DEC, D])
    nv_s = dout("nv_s", [2, DEC, D])
    nlf_s = dout("nlf_s", [2, DEC, NH_A])
    nssm_s = dout("nssm_s", [2, 2048, 128])
    nconv_s = dout("nconv_s", [2, 128, 32, 3])

    wb_in = dint("wb_in", [2, D, DIN])
    wb_m = dint("wb_m", [2, 2048, D])
    wb_a = dint("wb_a", [2, D, D])
    wb_o = dint("wb_o", [2, D, D])
    ktd = dint("ktd", [2, NCHP, 128, 8, 128])
    vd = dint("vd", [2, NCHP, 128, D])

    S = Sched(nc)
    import os as _os
    _KSTOP = int(_os.environ.get("KSTOP", "0"))
    DW = DBuf("weights")
    DKT = [DBuf("ktd0"), DBuf("ktd1")]
    DV = [DBuf("vd0"), DBuf("vd1")]

    ident = S.sbuf("ident", [128, 128], F32)
    tri_f = S.sbuf("tri_f", [128, 128], F32)
    gt_f = S.sbuf("gt_f", [128, 128], F32)
    ones_f = S.sbuf("ones_f", [128, 128], F32)
    maskneg = S.sbuf("maskneg", [128, 128], F32)
    tri_b = S.sbuf("tri_b", [128, 128], BF16)
    ones_b = S.sbuf("ones_b", [128, 64], BF16)
    selh = S.sbuf("selh", [80, 16, 128], BF16)
    pbc = S.sbuf("pbc_sb", [128, 2 * NPB_L], F32)
    ppp = S.sbuf("ppp_sb", [128, 2 * NPP_L], F32)
    A_bc = S.sbuf("A_bc", [128, 2, 32], F32)

    xres = [S.sbuf(f"xres{i}", [128, D], F32) for i in range(2)]
    xT = S.sbuf("xT", [128, 8, 128], BF16)
    wbufs = Rot([S.sbuf(f"wbuf{i}", [128, 4096], BF16) for i in range(4)])
    qT = S.sbuf("qT", [128, 8, 128], BF16)
    sgT = S.sbuf("sgT", [128, 8, 128], F32)
    yaT = S.sbuf("yaT", [128, 8, 128], BF16)
    ptr = Rot([S.sbuf(f"pt{i}", [128, 128], BF16) for i in range(6)])
    kvst = Rot([S.sbuf(f"kvst{i}", [128, D], F32) for i in range(2)])
    ktcur = Rot([S.sbuf(f"ktcur{i}", [128, 8, 128], BF16) for i in range(2)])
    vcur = Rot([S.sbuf(f"vcur{i}", [128, D], BF16) for i in range(2)])
    ktl = Rot([S.sbuf(f"ktl{i}", [128, 8, 128], BF16) for i in range(3)])
    vl = Rot([S.sbuf(f"vl{i}", [128, D], BF16) for i in range(3)])
    negc = [S.sbuf(f"negc{l}", [128, NCH, 16], F32) for l in range(2)]
    ccar = [S.sbuf(f"ccar{l}", [128, 16], F32) for l in range(2)]
    cTb = S.sbuf("cTb", [80, 128], BF16)
    cwide = S.sbuf("cwide", [128, 80], F32)
    sm = S.sbuf("sm", [128, 16, 32], F32)
    ST = [S.sbuf(f"ST{l}", [128, 2048], F32) for l in range(2)]
    STb = S.sbuf("STb", [128, 2048], BF16)
    cprev = [S.sbuf(f"cprev{l}", [128, 32, 3], F32) for l in range(2)]
    xpad = Rot([S.sbuf(f"xpad{i}", [128, 132], F32) for i in range(3)])
    cva = Rot([S.sbuf(f"cva{i}", [128, 128], F32) for i in range(3)])
    cvb = Rot([S.sbuf(f"cvb{i}", [128, 128], F32) for i in range(3)])
    xdt = S.sbuf("xdt", [128, 2048], BF16)
    xw = S.sbuf("xw", [128, 2048], BF16)
    xD = S.sbuf("xD", [128, 2048], F32)
    ybuf = S.sbuf("ybuf", [128, 2048], F32)
    BT = S.sbuf("BT", [128, 8, 128], BF16)
    CT = S.sbuf("CT", [128, 8, 128], BF16)
    Btok = S.sbuf("Btok", [128, 8, 128], BF16)
    cbm = S.sbuf("cbm", [128, 8, 128], F32)
    xsB = cbm
    rhsall = Rot([S.sbuf(f"rhsall{i}", [128, 4, 128], F32) for i in range(3)])
    eseg = Rot([S.sbuf(f"eseg{i}", [128, 4, 128], F32) for i in range(3)])
    MTt = Rot([S.sbuf(f"MT{i}", [128, 4, 128], BF16) for i in range(4)])
    t256 = Rot([S.sbuf(f"t256_{i}", [128, 256], F32) for i in range(2)])
    t512 = Rot([S.sbuf(f"t512_{i}", [128, 512], F32) for i in range(2)])
    ymT = S.sbuf("ymT", [128, 16, 128], BF16)
    gaT = S.sbuf("gaT", [128, 8, 128], F32)
    gmT = S.sbuf("gmT", [128, 8, 128], F32)
    uTf = sgT
    uT = S.sbuf("uT", [128, 8, 128], BF16)
    pre = ybuf
    clft = S.sbuf("clft", [128, max(NCHS, 1), 16], F32)
    castb = Buf("castb")

    banks = [S.psum(f"bank{i}", [128, 512], F32) for i in range(8)]
    psA = Rot(banks[0:4])
    ps6 = Rot(banks[0:6])
    psAll = Rot(banks)
    ybk = Rot(banks[4:8])
    oTb = banks[4:6]
    rsb = banks[6:8]

    def PB(l, off, n):
        return pbc.t[:, l * NPB_L + off: l * NPB_L + off + n]

    def PP(l, off, n=1):
        return ppp.t[:, l * NPP_L + off: l * NPP_L + off + n]

    def mm(out, lhsT, rhs, start, stop, reads, writes, **kw):
        S.op("pe", lambda e: e.matmul(out, lhsT=lhsT, rhs=rhs, start=start, stop=stop, **kw), reads=reads, writes=writes)

    def tp(out, in_, n_in_part, reads, writes):
        S.op("pe", lambda e: e.transpose(out=out, in_=in_, identity=ident.t[0:n_in_part, 0:n_in_part]), reads=list(reads) + [ident], writes=writes)

    def act(out, in_, func, reads, writes, **kw):
        S.op("act", lambda e: e.activation(out=out, in_=in_, func=func, **kw), reads=reads, writes=writes)

    def tt(eng, out, in0, in1, op, reads, writes):
        S.op(eng, lambda e: e.tensor_tensor(out=out, in0=in0, in1=in1, op=op), reads=reads, writes=writes)

    def ts(eng, out, in0, s1, s2, op0, op1, reads, writes):
        if op1 is None:
            S.op(eng, lambda e: e.tensor_scalar(out=out, in0=in0, scalar1=s1, scalar2=None, op0=op0), reads=reads, writes=writes)
        else:
            S.op(eng, lambda e: e.tensor_scalar(out=out, in0=in0, scalar1=s1, scalar2=s2, op0=op0, op1=op1), reads=reads, writes=writes)

    def cp(eng, out, in_, reads, writes):
        if eng == "act":
            S.op(eng, lambda e: e.activation(out=out, in_=in_, func=AF.Copy), reads=reads, writes=writes)
        else:
            S.op(eng, lambda e: e.tensor_copy(out=out, in_=in_), reads=reads, writes=writes)

    def ms(eng, ap, val, writes):
        S.op(eng, lambda e: e.memset(ap, val), writes=writes)

    ms("pool", ident.t[:, :], 0.0, [ident])
    S.op("pool", lambda e: e.affine_select(out=ident.t[:, :], in_=ident.t[:, :], pattern=[[-1, 128]], compare_op=ALU.not_equal,
                                           fill=1.0, base=0, channel_multiplier=1), reads=[ident], writes=[ident])
    ms("pool", tri_f.t[:, :], 1.0, [tri_f])
    S.op("pool", lambda e: e.affine_select(out=tri_f.t[:, :], in_=tri_f.t[:, :], pattern=[[1, 128]], compare_op=ALU.is_ge,
                                           fill=0.0, base=0, channel_multiplier=-1), reads=[tri_f], writes=[tri_f])
    ms("pool", gt_f.t[:, :], 1.0, [gt_f])
    S.op("pool", lambda e: e.affine_select(out=gt_f.t[:, :], in_=gt_f.t[:, :], pattern=[[-1, 128]], compare_op=ALU.is_gt,
                                           fill=0.0, base=0, channel_multiplier=1), reads=[gt_f], writes=[gt_f])
    ms("pool", ones_f.t[:, :], 1.0, [ones_f])
    ms("pool", maskneg.t[:, :], 0.0, [maskneg])
    S.op("pool", lambda e: e.affine_select(out=maskneg.t[:, :], in_=maskneg.t[:, :], pattern=[[1, 128]], compare_op=ALU.is_ge,
                                           fill=-30000.0, base=0, channel_multiplier=-1), reads=[maskneg], writes=[maskneg])
    ms("pool", ones_b.t[:, :], 1.0, [ones_b])
    cp("pool", tri_b.t[:, :], tri_f.t[:, :], [tri_f], [tri_b])
    ms("pool", selh.t[:, :, :], 0.0, [selh])
    for p0 in (0, 64):
        S.op("pool", lambda e, p0=p0: e.affine_select(out=selh.t[p0:p0 + 16, :, :], in_=selh.t[p0:p0 + 16, :, :], pattern=[[1, 16], [0, 128]], compare_op=ALU.not_equal,
                                                      fill=1.0, base=0, channel_multiplier=-1), reads=[selh], writes=[selh])
    ms("pool", cwide.t[:, :], 0.0, [cwide])
    S.dma("sp", pbc.t[:, :], pbc_d, writes=[pbc])
    S.dma("sp", ppp.t[:, :], ppp_d, writes=[ppp])
    for l in range(2):
        act(A_bc.t[:, l, :], PB(l, 32, 32), AF.Exp, [pbc], [A_bc])
    ts("dve", A_bc.t[:, :, :], A_bc.t[:, :, :], -1.0, None, ALU.mult, None, [A_bc], [A_bc])

    for l in range(2):
        for r0 in range(0, D, 256):
            S.dma("pool", wb_in[l, r0:r0 + 256, :].rearrange("r (a b) -> r a b", b=1542),
                  w_in[l, r0:r0 + 256, :].rearrange("r (a b) -> r a b", b=1542), writes=[castb], dram_w=DW, serialize=True)
        for r0 in range(0, 2048, 1024):
            S.dma("pool", wb_m[l, r0:r0 + 1024, :], w_br_m[l, r0:r0 + 1024, :], writes=[castb], dram_w=DW, serialize=True)
        S.dma("pool", wb_a[l, :, :], w_br_a[l, :, :], writes=[castb], dram_w=DW, serialize=True)
        S.dma("pool", wb_o[l, :, :], w_out[l, :, :], writes=[castb], dram_w=DW, serialize=True)

    def wload(src2d, nk, c0, ncols):
        wt = wbufs.next()
        view = wt.t[:, 0:nk * ncols].rearrange("p (k n) -> p k n", n=ncols)
        S.dma("sp", view, src2d[:, c0:c0 + ncols].rearrange("(k p) n -> p k n", p=128), writes=[wt], dram_r=DW)
        return wt, view

    class _Stop(Exception):
        pass

    def stage(n):
        S.phase = f"st{n}"
        if _KSTOP == n:
            raise _Stop()

    def tile_layer(l, TS, xr, chunk_idx, past_chunks, out_k, out_v, out_lf):
        Win = wb_in[l]
        for g in range(2):
            bk = psAll.next()
            for c in range(4):
                tp(bk.t[:, c * 128:c * 128 + TS], xr.t[0:TS, (4 * g + c) * 128:(4 * g + c + 1) * 128], TS, [xr], [bk])
            act(xT.t[:, 4 * g:4 * g + 4, 0:TS], bk.t[:, :].rearrange("p (c t) -> p c t", t=128)[:, :, 0:TS], AF.Copy, [bk], [xT])

        def proj_fm(src2d, c0, nchunk, nk, rhsT, evac):
            done = 0
            while done < nchunk:
                n = min(4, nchunk - done)
                wt, wv = wload(src2d, nk, c0 + done * 128, n * 128)
                for jj in range(n):
                    bk = psAll.next()
                    for kc in range(nk):
                        mm(bk.t[:, 0:TS], wv[:, kc, jj * 128:(jj + 1) * 128], rhsT.t[:, kc, 0:TS], kc == 0, kc == nk - 1, [wt, rhsT], [bk])
                    evac(done + jj, bk)
                done += n

        def proj_tm(src2d, c0, ncols, nk, lhsT, evac, wpre=None):
            if wpre is None:
                wt, wv = wload(src2d, nk, c0, ncols)
            else:
                wt, wv = wpre
            bk = psAll.next()
            for kc in range(nk):
                mm(bk.t[0:TS, 0:ncols], lhsT.t[:, kc, 0:TS], wv[:, kc, 0:ncols], kc == 0, kc == nk - 1, [wt, lhsT], [bk])
            evac(bk)

        stage(10)
        proj_fm(Win, C_Q, 8, 8, xT, lambda j, bk: act(qT.t[:, j, 0:TS], bk.t[:, 0:TS], AF.Copy, [bk], [qT], scale=0.125))
        ktc = ktcur.next()
        vc = vcur.next()
        kst_t = kvst.next()
        for blk in range(2):
            proj_tm(Win, C_K + blk * 512, 512, 8, xT, lambda bk, blk=blk: cp("dve", kst_t.t[0:TS, blk * 512:(blk + 1) * 512], bk.t[0:TS, 0:512], [bk], [kst_t]))
        stage(11)
        S.dma("pool", out_k, kst_t.t[0:TS, :], reads=[kst_t], final=True)
        stage(12)
        vst_t = kvst.next()
        for blk in range(2):
            def ev(bk, blk=blk):
                cp("dve", vst_t.t[0:TS, blk * 512:(blk + 1) * 512], bk.t[0:TS, 0:512], [bk], [vst_t])
                act(vc.t[0:TS, blk * 512:(blk + 1) * 512], bk.t[0:TS, 0:512], AF.Copy, [bk], [vc])
            proj_tm(Win, C_V + blk * 512, 512, 8, xT, ev)
        S.dma("pool", out_v, vst_t.t[0:TS, :], reads=[vst_t], final=True)
        for g2 in range(2):
            bk = psAll.next()
            for cc in range(4):
                j = g2 * 4 + cc
                tp(bk.t[:, cc * 128:cc * 128 + TS], kst_t.t[0:TS, j * 128:(j + 1) * 128], TS, [kst_t], [bk])
            act(ktc.t[:, g2 * 4:g2 * 4 + 4, 0:TS], bk.t[:, :].rearrange("p (c t) -> p c t", t=128)[:, :, 0:TS], AF.Copy, [bk], [ktc])
        if past_chunks is not None and chunk_idx is not None and TS == 128 and chunk_idx < NCHP - 1:
            S.dma("pool", ktd[l, chunk_idx], ktc.t[:, :, :], reads=[ktc], dram_w=DKT[l], final=True)
            S.dma("pool", vd[l, chunk_idx], vc.t[:, :], reads=[vc], dram_w=DV[l], final=True)
        stage(13)
        ci = chunk_idx
        LSP, LF, CC = sm.t[:, 0, 0:16], sm.t[:, 1, 0:16], sm.t[:, 2, 0:16]

        def ev_f(bk):
            tt("dve", sm.t[0:TS, 0, 0:16], bk.t[0:TS, 0:16], PB(l, 96, 16)[0:TS, :], ALU.add, [bk, pbc], [sm])
            act(sm.t[0:TS, 0, 0:16], sm.t[0:TS, 0, 0:16], AF.Exp, [sm], [sm], scale=-1.0)
            act(sm.t[0:TS, 0, 0:16], sm.t[0:TS, 0, 0:16], AF.Ln, [sm], [sm], bias=1.0, scale=1.0)
            ts("dve", sm.t[0:TS, 1, 0:16], sm.t[0:TS, 0, 0:16], -1.0, None, ALU.mult, None, [sm], [sm])
        proj_tm(Win, C_F, 16, 8, xT, ev_f)
        proj_fm(Win, C_G, 8, 8, xT, lambda j, bk: act(sgT.t[:, j, 0:TS], bk.t[:, 0:TS], AF.Silu, [bk], [sgT]))
        S.dma("pool", out_lf, sm.t[0:TS, 1, 0:16], reads=[sm], final=True)
        cumsum_chunk(l, TS, sm.t[0:TS, 1, 0:16], sm, ci)
        ts("dve", cwide.t[0:TS, 0:16], negc[l].t[0:TS, ci, :], -1.0, None, ALU.mult, None, [negc[l]], [cwide])
        ts("dve", cwide.t[0:TS, 64:80], negc[l].t[0:TS, ci, :], -1.0, None, ALU.mult, None, [negc[l]], [cwide])
        bk = psAll.next()
        tp(bk.t[0:80, 0:TS], cwide.t[0:TS, 0:80], TS, [cwide], [bk])
        act(cTb.t[0:80, 0:TS], bk.t[0:80, 0:TS], AF.Copy, [bk], [cTb])
        stage(14)
        stage(15)
        chunks = []
        for pc_i, pc in enumerate(past_chunks):
            chunks.append((pc, pc_i, 128, False))
        chunks.append((lambda: (ktc, vc), ci, TS, True))
        for half in range(2):
          ob_h = banks[6]
          rb_h = banks[7]
          for b_ in (ob_h, rb_h):
            ms("dve", b_.t[:, :], 0.0, [b_])
          items = [(cix, j) for cix in range(len(chunks)) for j in range(half * 4, half * 4 + 4)]
          loaded = {}

          def stA(it):
              cix, j = it
              getter, cidx, nk, diag = chunks[cix]
              if cix not in loaded:
                  loaded[cix] = getter()
              kt_t, v_t = loaded[cix]
              sbs = [ps6.next(), ps6.next()]
              for e_ in range(2):
                  hb = 64 * e_
                  mm(sbs[e_].t[0:nk, 0:TS], kt_t.t[hb:hb + 64, j, 0:nk], qT.t[hb:hb + 64, j, 0:TS], True, False, [kt_t, qT], [sbs[e_]])
              for e_ in range(2):
                  hb = 64 * e_
                  h = 2 * j + e_
                  mm(sbs[e_].t[0:nk, 0:TS], selh.t[hb:hb + 16, h, 0:nk], cTb.t[hb:hb + 16, 0:TS], False, True, [selh, cTb], [sbs[e_]])
              pts_ = []
              for e_ in range(2):
                  h = 2 * j + e_
                  sb = sbs[e_]
                  pt = ptr.next()
                  if diag:
                      tm_ = cva.next()
                      S.op("dve", lambda e, tm_=tm_, sb=sb, h=h, cidx=cidx, nk=nk: e.scalar_tensor_tensor(
                          out=tm_.t[0:nk, 0:TS], in0=sb.t[0:nk, 0:TS], scalar=negc[l].t[0:nk, cidx, h:h + 1], in1=maskneg.t[0:nk, 0:TS],
                          op0=ALU.add, op1=ALU.add), reads=[sb, negc[l], maskneg], writes=[tm_])
                      act(pt.t[0:nk, 0:TS], tm_.t[0:nk, 0:TS], AF.Exp, [tm_], [pt])
                  else:
                      act(pt.t[0:nk, 0:TS], sb.t[0:nk, 0:TS], AF.Exp, [sb, negc[l]], [pt], bias=negc[l].t[0:nk, cidx, h:h + 1], scale=1.0)
                  pts_.append(pt)
              return pts_

          def stB(it, pts_):
              cix, j = it
              getter, cidx, nk, diag = chunks[cix]
              kt_t, v_t = loaded[cix]
              ob = ob_h
              rb = rb_h
              col = (j % 4) * 128
              for e_ in range(2):
                  hb = 64 * e_
                  h = 2 * j + e_
                  mm(ob.t[hb:hb + 64, col:col + TS], v_t.t[0:nk, h * 64:(h + 1) * 64], pts_[e_].t[0:nk, 0:TS], False, False, [v_t, pts_[e_]], [ob], skip_group_check=True)
              for e_ in range(2):
                  hb = 64 * e_
                  mm(rb.t[hb:hb + 64, col:col + TS], ones_b.t[0:nk, 0:64], pts_[e_].t[0:nk, 0:TS], False, False, [ones_b, pts_[e_]], [rb], skip_group_check=True)

          LOOK = 2
          pts = {}
          for i in range(len(items) + LOOK):
              if i < len(items):
                  pts[i] = stA(items[i])
              if i - LOOK >= 0:
                  stB(items[i - LOOK], pts.pop(i - LOOK))
          tmp = t512.next()
          tv = tmp.t[:, :].rearrange("p (c t) -> p c t", t=128)[:, :, 0:TS]
          S.op("dve", lambda e, tv=tv, rb_h=rb_h: e.reciprocal(out=tv, in_=rb_h.t[:, :].rearrange("p (c t) -> p c t", t=128)[:, :, 0:TS]), reads=[rb_h], writes=[tmp])
          tt("dve", tv, ob_h.t[:, :].rearrange("p (c t) -> p c t", t=128)[:, :, 0:TS], tv, ALU.mult, [ob_h, tmp], [tmp])
          tt("pool", yaT.t[:, 4 * half:4 * half + 4, 0:TS], tv, sgT.t[:, 4 * half:4 * half + 4, 0:TS], ALU.mult, [tmp, sgT], [yaT])

        stage(16)
        DT, DTA, ACUM, EA, WEND, EEND, TMPD = (sm.t[:, 3, :], sm.t[:, 4, :], sm.t[:, 5, :], sm.t[:, 6, :], sm.t[:, 7, :], sm.t[:, 8, :], sm.t[:, 9, :])

        def ev_dt(bk):
            tt("dve", DT[0:TS], bk.t[0:TS, 0:32], PB(l, 0, 32)[0:TS, :], ALU.add, [bk, pbc], [sm])
            act(DT[0:TS], DT[0:TS], AF.Exp, [sm], [sm])
            act(DT[0:TS], DT[0:TS], AF.Ln, [sm], [sm], bias=1.0, scale=1.0)
            tt("dve", DTA[0:TS], DT[0:TS], A_bc.t[0:TS, l, :], ALU.mult, [sm, A_bc], [sm])
        proj_tm(Win, C_DT, 32, 8, xT, ev_dt)
        cp("act", STb.t[:, :], ST[l].t[:, :], [ST[l]], [STb])

        stage(17)

        pend = []

        def conv_y(c, tb):
            if c < 16:
                act(ybuf.t[:, c * 128:c * 128 + TS], tb.t[:, 0:TS], AF.Silu, [tb], [ybuf])
            elif c < 24:
                g = c - 16
                act(xsB.t[:, g, 0:TS], tb.t[:, 0:TS], AF.Silu, [tb], [xsB])
                cp("pool", BT.t[:, g, 0:TS], xsB.t[:, g, 0:TS], [xsB], [BT])
            else:
                g = c - 24
                act(CT.t[:, g, 0:TS], tb.t[:, 0:TS], AF.Silu, [tb], [CT])

        nxt_xp = [None]

        def ev_conv(c, bk):
            if nxt_xp[0] is None:
                xp_ = xpad.next()
                cp("pool", xp_.t[:, 0:3], cprev[l].t[:, c, :], [cprev[l]], [xp_])
            else:
                xp_ = nxt_xp[0]
            ta = cva.next()
            tb = cvb.next()
            act(xp_.t[:, 3:3 + TS], bk.t[:, 0:TS], AF.Copy, [bk], [xp_])
            act(ta.t[:, 0:TS], bk.t[:, 0:TS], AF.Identity, [bk, ppp], [ta], scale=PP(l, c * 4 + 3), bias=PP(l, 128 + c))
            if c + 1 < 32:
                nxp = xpad.next()
                cp("pool", nxp.t[:, 0:3], cprev[l].t[:, c + 1, :], [cprev[l]], [nxp])
                nxt_xp[0] = nxp
            else:
                nxt_xp[0] = None
            S.op("dve", lambda e: e.scalar_tensor_tensor(out=tb.t[:, 0:TS], in0=xp_.t[:, 0:TS], scalar=PP(l, c * 4 + 0), in1=ta.t[:, 0:TS], op0=ALU.mult, op1=ALU.add), reads=[xp_, ta, ppp], writes=[tb])
            S.op("dve", lambda e: e.scalar_tensor_tensor(out=ta.t[:, 0:TS], in0=xp_.t[:, 1:1 + TS], scalar=PP(l, c * 4 + 1), in1=tb.t[:, 0:TS], op0=ALU.mult, op1=ALU.add), reads=[xp_, tb, ppp], writes=[ta])
            S.op("dve", lambda e: e.scalar_tensor_tensor(out=tb.t[:, 0:TS], in0=xp_.t[:, 2:2 + TS], scalar=PP(l, c * 4 + 2), in1=ta.t[:, 0:TS], op0=ALU.mult, op1=ALU.add), reads=[xp_, ta, ppp], writes=[tb])
            cp("pool", cprev[l].t[:, c, :], xp_.t[:, TS:TS + 3], [xp_], [cprev[l]])
            pend.append((c, tb))
            if len(pend) > 1:
                conv_y(*pend.pop(0))
        proj_fm(Win, C_X, 32, 8, xT, ev_conv)
        while pend:
            conv_y(*pend.pop(0))
        bk1 = psAll.next()
        mm(bk1.t[0:TS, 0:32], tri_f.t[0:TS, 0:TS], DTA[0:TS], True, True, [tri_f, sm], [bk1])
        bk2 = psAll.next()
        mm(bk2.t[0:128, 0:32], ones_f.t[0:TS, 0:128], DTA[0:TS], True, True, [ones_f, sm], [bk2])
        act(ACUM[0:TS], bk1.t[0:TS, 0:32], AF.Copy, [bk1], [sm])
        act(EA[0:TS], bk1.t[0:TS, 0:32], AF.Exp, [bk1], [sm])
        act(EEND, bk2.t[:, 0:32], AF.Exp, [bk2], [sm])
        tt("dve", TMPD[0:TS], bk2.t[0:TS, 0:32], ACUM[0:TS], ALU.subtract, [bk2, sm], [sm])
        act(TMPD[0:TS], TMPD[0:TS], AF.Exp, [sm], [sm])
        tt("dve", WEND[0:TS], TMPD[0:TS], DT[0:TS], ALU.mult, [sm], [sm])
        for q4 in range(4):
            xb = psAll.next()
            for cc in range(4):
                c = q4 * 4 + cc
                tp(xb.t[0:TS, cc * 128:(cc + 1) * 128], ybuf.t[:, c * 128:c * 128 + TS], 128, [ybuf], [xb])
            c0 = q4 * 512
            h0 = c0 // 64
            xv = xb.t[0:TS, :].rearrange("p (h d) -> p h d", d=64)
            tt("dve", xdt.t[0:TS, c0:c0 + 512].rearrange("p (h d) -> p h d", d=64), xv, DT[0:TS, h0:h0 + 8].unsqueeze(2).broadcast_to([TS, 8, 64]), ALU.mult, [xb, sm], [xdt])
            tt("dve", xw.t[0:TS, c0:c0 + 512].rearrange("p (h d) -> p h d", d=64), xv, WEND[0:TS, h0:h0 + 8].unsqueeze(2).broadcast_to([TS, 8, 64]), ALU.mult, [xb, sm], [xw])
            tt("dve", xD.t[0:TS, c0:c0 + 512].rearrange("p (h d) -> p h d", d=64), xv, PB(l, 64, 32)[0:TS, h0:h0 + 8].unsqueeze(2).broadcast_to([TS, 8, 64]), ALU.mult, [xb, pbc], [xD])
        for q4 in range(2):
            xb = psAll.next()
            for cc in range(4):
                g = q4 * 4 + cc
                tp(xb.t[0:TS, cc * 128:(cc + 1) * 128], xsB.t[:, g, 0:TS], 128, [xsB], [xb])
            act(Btok.t[0:TS, q4 * 4:q4 * 4 + 4, :], xb.t[0:TS, :].rearrange("p (g n) -> p g n", n=128), AF.Copy, [xb], [Btok])

        stage(18)
        for half in range(2):
            bk = psAll.next()
            for gg in range(4):
                g = half * 4 + gg
                mm(bk.t[0:TS, gg * 128:gg * 128 + TS], BT.t[:, g, 0:TS], CT.t[:, g, 0:TS], True, True, [BT, CT], [bk])
            tt("dve", cbm.t[0:TS, half * 4:half * 4 + 4, 0:TS], bk.t[0:TS, :].rearrange("p (g t) -> p g t", t=128)[:, :, 0:TS],
               tri_f.t[0:TS, 0:TS].unsqueeze(1).broadcast_to([TS, 4, TS]), ALU.mult, [bk, tri_f], [cbm])
        def s1(g):
            ra = rhsall.next()
            tt("pool", ra.t[0:TS, :, 0:TS], DTA[0:TS, 4 * g:4 * g + 4].unsqueeze(2).broadcast_to([TS, 4, TS]),
               tri_f.t[0:TS, 0:TS].unsqueeze(1).broadcast_to([TS, 4, TS]), ALU.mult, [sm, tri_f], [ra])
            bk = psA.next()
            if TS == 128:
                mm(bk.t[0:TS, 0:512], gt_f.t[0:TS, 0:TS], ra.t[0:TS, :, :].rearrange("p r t -> p (r t)"), True, True, [gt_f, ra], [bk])
            else:
                for r in range(4):
                    mm(bk.t[0:TS, r * 128:r * 128 + TS], gt_f.t[0:TS, 0:TS], ra.t[0:TS, r, 0:TS], True, True, [gt_f, ra], [bk])
            es = eseg.next()
            act(es.t[0:TS, :, 0:TS], bk.t[0:TS, :].rearrange("p (r t) -> p r t", t=128)[:, :, 0:TS], AF.Exp, [bk], [es])
            mt = MTt.next()
            tt("dve", mt.t[0:TS, :, 0:TS], es.t[0:TS, :, 0:TS], cbm.t[0:TS, g, 0:TS].unsqueeze(1).broadcast_to([TS, 4, TS]), ALU.mult, [es, cbm], [mt])
            return mt

        def s2(g, mt):
            yb = ybk.next()
            for r in range(4):
                h = 4 * g + r
                mm(yb.t[0:TS, r * 64:(r + 1) * 64], mt.t[0:TS, r, 0:TS], xdt.t[0:TS, h * 64:(h + 1) * 64], True, True, [mt, xdt], [yb])
            mm(yb.t[0:TS, 256:512], CT.t[:, g, 0:TS], STb.t[:, 256 * g:256 * g + 256], True, True, [CT, STb], [yb])
            t1 = t256.next()
            tt("dve", t1.t[0:TS, :].rearrange("p (r d) -> p r d", d=64), yb.t[0:TS, 256:512].rearrange("p (r d) -> p r d", d=64),
               EA[0:TS, 4 * g:4 * g + 4].unsqueeze(2).broadcast_to([TS, 4, 64]), ALU.mult, [yb, sm], [t1])
            tt("dve", t1.t[0:TS, :], yb.t[0:TS, 0:256], t1.t[0:TS, :], ALU.add, [yb, t1], [t1])
            tt("dve", ybuf.t[0:TS, 256 * g:256 * g + 256], t1.t[0:TS, :], xD.t[0:TS, 256 * g:256 * g + 256], ALU.add, [t1, xD], [ybuf])

        mts = {}
        for g in range(8 + 2):
            if g < 8:
                mts[g] = s1(g)
            if g - 2 >= 0:
                s2(g - 2, mts.pop(g - 2))
        stage(19)
        for gp in range(4):
            bk = psAll.next()
            for gg in range(2):
                g = gp * 2 + gg
                mm(bk.t[:, gg * 256:(gg + 1) * 256], Btok.t[0:TS, g, :], xw.t[0:TS, 256 * g:256 * g + 256], True, True, [Btok, xw], [bk])
            sv = ST[l].t[:, gp * 512:(gp + 1) * 512]
            tt("pool", sv.rearrange("p (h d) -> p h d", d=64), sv.rearrange("p (h d) -> p h d", d=64),
               EEND[:, gp * 8:gp * 8 + 8].unsqueeze(2).broadcast_to([128, 8, 64]), ALU.mult, [ST[l], sm], [ST[l]])
            tt("dve", sv, sv, bk.t[:, :], ALU.add, [ST[l], bk], [ST[l]])
        stage(20)
        for blk in range(4):
            def ev_z(bk, blk=blk):
                tz = t512.next()
                act(tz.t[0:TS, :], bk.t[0:TS, :], AF.Silu, [bk], [tz])
                tt("dve", ybuf.t[0:TS, blk * 512:(blk + 1) * 512], ybuf.t[0:TS, blk * 512:(blk + 1) * 512], tz.t[0:TS, :], ALU.mult, [ybuf, tz], [ybuf])
            proj_tm(Win, C_Z + blk * 512, 512, 8, xT, ev_z)
        SS, RST = sm.t[:, 10, 0:8], sm.t[:, 11, 0:8]
        tt("pool", xD.t[0:TS, :], ybuf.t[0:TS, :], ybuf.t[0:TS, :], ALU.mult, [ybuf], [xD])
        S.op("dve", lambda e: e.tensor_reduce(out=SS[0:TS], in_=xD.t[0:TS, :].rearrange("p (g c) -> p g c", c=256), axis=mybir.AxisListType.X, op=ALU.add), reads=[xD], writes=[sm])
        act(RST[0:TS], SS[0:TS], AF.Sqrt, [sm], [sm], bias=1e-5, scale=1.0 / 256.0)
        S.op("dve", lambda e: e.reciprocal(out=RST[0:TS], in_=RST[0:TS]), reads=[sm], writes=[sm])
        tt("dve", ybuf.t[0:TS, :].rearrange("p (g c) -> p g c", c=256), ybuf.t[0:TS, :].rearrange("p (g c) -> p g c", c=256),
           RST[0:TS].unsqueeze(2).broadcast_to([TS, 8, 256]), ALU.mult, [ybuf, sm], [ybuf])
        proj_fm(Win, C_MA, 8, 8, xT, lambda j, bk: act(gaT.t[:, j, 0:TS], bk.t[:, 0:TS], AF.Sigmoid, [bk, ppp], [gaT], bias=PP(l, 176 + 8 + j), scale=1.0))
        proj_fm(Win, C_MM, 8, 8, xT, lambda j, bk: act(gmT.t[:, j, 0:TS], bk.t[:, 0:TS], AF.Sigmoid, [bk, ppp], [gmT], bias=PP(l, 176 + j), scale=1.0))
        proj_fm(wb_a[l], 0, 8, 8, yaT, lambda j, bk: tt("dve", uTf.t[:, j, 0:TS], bk.t[:, 0:TS], gaT.t[:, j, 0:TS], ALU.mult, [bk, gaT], [uTf]))
        for q4 in range(4):
            bk = psAll.next()
            for cc in range(4):
                c = q4 * 4 + cc
                tp(bk.t[:, cc * 128:cc * 128 + TS], ybuf.t[0:TS, c * 128:(c + 1) * 128], TS, [ybuf], [bk])
            tt("dve", ymT.t[:, q4 * 4:q4 * 4 + 4, 0:TS], bk.t[:, :].rearrange("p (c t) -> p c t", t=128)[:, :, 0:TS],
               PP(l, 160 + q4 * 4, 4).unsqueeze(2).broadcast_to([128, 4, TS]), ALU.mult, [bk, ppp], [ymT])

        stage(21)

        def ev_m(j, bk):
            tmp = cva.next()
            tt("dve", tmp.t[:, 0:TS], bk.t[:, 0:TS], gmT.t[:, j, 0:TS], ALU.mult, [bk, gmT], [tmp])
            tt("pool", uT.t[:, j, 0:TS], tmp.t[:, 0:TS], uTf.t[:, j, 0:TS], ALU.add, [tmp, uTf], [uT])
        for blk in range(4):
            wt, wv = wload(wb_m[l], 16, blk * 256, 256)
            for jj in range(2):
                j = blk * 2 + jj
                bk = psAll.next()
                for kc in range(16):
                    mm(bk.t[:, 0:TS], wv[:, kc, jj * 128:(jj + 1) * 128], ymT.t[:, kc, 0:TS], kc == 0, kc == 15, [wt, ymT], [bk])
                ev_m(j, bk)
        for blk in range(2):
            def ev_o(bk, blk=blk):
                S.op("dve", lambda e: e.scalar_tensor_tensor(out=pre.t[0:TS, blk * 512:(blk + 1) * 512], in0=xr.t[0:TS, blk * 512:(blk + 1) * 512], scalar=float(DN_ALPHA),
                                                             in1=bk.t[0:TS, 0:512], op0=ALU.mult, op1=ALU.add), reads=[xr, bk], writes=[pre])
            proj_tm(wb_o[l], blk * 512, 512, 8, uT, ev_o)
        STATS, MV, RS2 = sm.t[:, 12, 0:12], sm.t[:, 13, 0:2], sm.t[:, 13, 2:3]
        S.op("dve", lambda e: e.bn_stats(out=STATS[0:TS, 0:6], in_=pre.t[0:TS, 0:512]), reads=[pre], writes=[sm])
        S.op("dve", lambda e: e.bn_stats(out=STATS[0:TS, 6:12], in_=pre.t[0:TS, 512:1024]), reads=[pre], writes=[sm])
        S.op("dve", lambda e: e.bn_aggr(out=MV[0:TS], in_=STATS[0:TS]), reads=[sm], writes=[sm])
        act(RS2[0:TS], MV[0:TS, 1:2], AF.Sqrt, [sm], [sm], bias=1e-5, scale=1.0)
        S.op("dve", lambda e: e.reciprocal(out=RS2[0:TS], in_=RS2[0:TS]), reads=[sm], writes=[sm])
        ts("dve", pre.t[0:TS, 0:D], pre.t[0:TS, 0:D], MV[0:TS, 0:1], RS2[0:TS], ALU.subtract, ALU.mult, [pre, sm], [pre])
        tt("pool", pre.t[0:TS, 0:D], pre.t[0:TS, 0:D], PB(l, 112, 1024)[0:TS, :], ALU.mult, [pre, pbc], [pre])
        tt("dve", xr.t[0:TS, :], pre.t[0:TS, 0:D], PB(l, 1136, 1024)[0:TS, :], ALU.add, [pre, pbc], [xr])

    def cumsum_chunk(l, TS, lf_ap, lf_tile, ci):
        bk1 = psAll.next()
        mm(bk1.t[0:TS, 0:16], tri_f.t[0:TS, 0:TS], lf_ap, True, True, [tri_f, lf_tile], [bk1])
        bk2 = psAll.next()
        mm(bk2.t[0:128, 0:16], ones_f.t[0:TS, 0:128], lf_ap, True, True, [ones_f, lf_tile], [bk2])
        tt("dve", negc[l].t[0:TS, ci, :], bk1.t[0:TS, 0:16], ccar[l].t[0:TS, :], ALU.add, [bk1, ccar[l]], [negc[l]])
        ts("dve", negc[l].t[0:TS, ci, :], negc[l].t[0:TS, ci, :], -1.0, None, ALU.mult, None, [negc[l]], [negc[l]])
        tt("dve", ccar[l].t[:, :], ccar[l].t[:, :], bk2.t[:, 0:16], ALU.add, [ccar[l], bk2], [ccar[l]])

    def flush_state(l, TS_unused, out_ssm, out_conv):
        for q4 in range(4):
            bk = psAll.next()
            for cc in range(4):
                c = q4 * 4 + cc
                tp(bk.t[:, cc * 128:(cc + 1) * 128], ST[l].t[:, c * 128:(c + 1) * 128], 128, [ST[l]], [bk])
            act(xD.t[:, q4 * 512:(q4 + 1) * 512].rearrange("p (c n) -> p c n", n=128), bk.t[:, :].rearrange("p (c n) -> p c n", n=128), AF.Copy, [bk], [xD])
        S.dma("pool", out_ssm.rearrange("(c p) n -> p c n", p=128), xD.t[:, :].rearrange("p (c n) -> p c n", n=128), reads=[xD], final=True)
        S.dma("pool", out_conv, cprev[l].t[:, :, :], reads=[cprev[l]], final=True)

    def _main_body():
        xi = 0
        for b in range(NB):
            for l in range(2):
                ms("pool", ST[l].t[:, :], 0.0, [ST[l]])
                ms("pool", cprev[l].t[:, :, :], 0.0, [cprev[l]])
                ms("pool", ccar[l].t[:, :], 0.0, [ccar[l]])
            for ci in range(NCHP):
                xr = xres[xi % 2]
                xi += 1
                S.dma("sp", xr.t[:, :], xp[b, ci * 128:(ci + 1) * 128, :], writes=[xr])
                for l in range(2):
                    def mk(c, l=l):
                        def g():
                            kt_t = ktl.next()
                            v_t = vl.next()
                            S.dma("sp", kt_t.t[:, :, :], ktd[l, c], writes=[kt_t], dram_r=DKT[l])
                            S.dma("sp", v_t.t[:, :], vd[l, c], writes=[v_t], dram_r=DV[l])
                            return kt_t, v_t
                        return g
                    tile_layer(l, 128, xr, ci, [mk(c) for c in range(ci)],
                               nk_p[l, b, ci * 128:(ci + 1) * 128, :], nv_p[l, b, ci * 128:(ci + 1) * 128, :], nlf_p[l, b, ci * 128:(ci + 1) * 128, :])
                    stage(30 + l)
                S.dma("pool", y_p[b, ci * 128:(ci + 1) * 128, :], xr.t[:, :], reads=[xr], final=True)
                stage(2)
            for l in range(2):
                flush_state(l, 128, nssm_p[l, b], nconv_p[l, b])
        stage(3)
        xr = xres[xi % 2]
        S.dma("sp", xr.t[0:DEC, :], xs, writes=[xr])
        S.dma("sp", cprev[0].t[:, :, :], sconv[:, 0], writes=[cprev[0]])
        S.dma("sp", cprev[1].t[:, :, :], sconv[:, 1], writes=[cprev[1]])
        for l in range(2):
            S.dma("sp", xD.t[:, :].rearrange("p (c n) -> p c n", n=128), sssm[l].rearrange("(c p) n -> p c n", p=128), writes=[xD])
            for q4 in range(4):
                bk = psAll.next()
                for cc in range(4):
                    c = q4 * 4 + cc
                    tp(bk.t[:, cc * 128:(cc + 1) * 128], xD.t[:, c * 128:(c + 1) * 128], 128, [xD], [bk])
                act(ST[l].t[:, q4 * 512:(q4 + 1) * 512], bk.t[:, :], AF.Copy, [bk], [ST[l]])
            ms("pool", ccar[l].t[:, :], 0.0, [ccar[l]])
            S.dma("sp", clft.t[:, :, :], clf[l].rearrange("(c p) h -> p c h", p=128), writes=[clft])
            for c in range(NCHS):
                cumsum_chunk(l, 128, clft.t[:, c, :], clft, c)

            def mk(c, l=l):
                def g():
                    ks = kvst.next()
                    S.dma("sp", ks.t[:, :], ck[l, c * 128:(c + 1) * 128, :], writes=[ks])
                    kt_t = ktl.next()
                    for g2 in range(2):
                        bk = psA.next()
                        for cc in range(4):
                            tp(bk.t[:, cc * 128:(cc + 1) * 128], ks.t[:, (g2 * 4 + cc) * 128:(g2 * 4 + cc + 1) * 128], 128, [ks], [bk])
                        act(kt_t.t[:, g2 * 4:g2 * 4 + 4, :], bk.t[:, :].rearrange("p (c t) -> p c t", t=128), AF.Copy, [bk], [kt_t])
                    v_t = vl.next()
                    S.dma("pool", v_t.t[:, :], cv[l, c * 128:(c + 1) * 128, :], writes=[v_t])
                    return kt_t, v_t
                return g
            tile_layer(l, DEC, xr, NCHS, [mk(c) for c in range(NCHS)], nk_s[l], nv_s[l], nlf_s[l])
            flush_state(l, DEC, nssm_s[l], nconv_s[l])
        S.dma("pool", y_s, xr.t[0:DEC, :], reads=[xr], final=True)

    try:
        stage(1)
        _main_body()
    except _Stop:
        pass
    S.emit()
    if _os.environ.get("KDUMP"):
        import json as _json
        _json.dump({e: [o.get("phase", "dma") for o in S.ops[e]] for e in ENGS}, open(_os.environ["KDUMP"], "w"))
    return nc


_CACHE = {}


def _host_params(conv_w, conv_b, dt_bias, a_log, d_skip, mnorm_w, b_f, b_merge, ln_g, ln_b):
    pb = np.zeros((2 * NPB_L,), np.float32)
    pp = np.zeros((128, 2 * NPP_L), np.float32)
    for l in range(2):
        o = l * NPB_L
        pb[o:o + 32] = dt_bias[l]
        pb[o + 32:o + 64] = a_log[l]
        pb[o + 64:o + 96] = d_skip[l]
        pb[o + 96:o + 112] = b_f[l]
        pb[o + 112:o + 1136] = ln_g[l]
        pb[o + 1136:o + 2160] = ln_b[l]
        o = l * NPP_L
        pp[:, o:o + 128] = conv_w[l].reshape(4, 32, 128).transpose(2, 1, 0).reshape(128, 128)
        pp[:, o + 128:o + 160] = conv_b[l].reshape(32, 128).T
        pp[:, o + 160:o + 176] = mnorm_w[l].reshape(16, 128).T
        pp[:, o + 176:o + 192] = b_merge[l].reshape(16, 128).T
    pbc = np.ascontiguousarray(np.broadcast_to(pb[None, :], (128, 2 * NPB_L)))
    return pbc, pp


def kernel(x_prompt, x_sample, cache_k, cache_v, cache_logf, state_ssm, state_conv, w_in, conv_w,
           conv_b, dt_bias, a_log, d_skip, mnorm_w, b_f, b_merge, w_br_m, w_br_a, w_out, ln_g, ln_b):
    f = lambda a: np.ascontiguousarray(np.asarray(a, dtype=np.float32))
    x_prompt, x_sample, cache_k, cache_v, cache_logf, state_ssm, state_conv = map(f, (x_prompt, x_sample, cache_k, cache_v, cache_logf, state_ssm, state_conv))
    w_in, w_br_m, w_br_a, w_out = map(f, (w_in, w_br_m, w_br_a, w_out))
    B, SEQ, _ = x_prompt.shape
    NCORE, DEC, _ = x_sample.shape
    PAST = cache_k.shape[2]
    NB = B // NCORE
    key = (SEQ, PAST, NB, DEC)
    if key not in _CACHE:
        _CACHE[key] = build(SEQ, PAST, NB, DEC)
    nc = _CACHE[key]
    pbc, pp = _host_params(*map(f, (conv_w, conv_b, dt_bias, a_log, d_skip, mnorm_w, b_f, b_merge, ln_g, ln_b)))
    in_maps = []
    for c in range(NCORE):
        in_maps.append({
            "xp": x_prompt[c * NB:(c + 1) * NB],
            "xs": x_sample[c],
            "ck": cache_k[:, c].reshape(2, PAST, D),
            "cv": cache_v[:, c].reshape(2, PAST, D),
            "clf": cache_logf[:, c],
            "sssm": state_ssm[:, c].reshape(2, 2048, 128),
            "sconv": np.ascontiguousarray(state_conv[:, c].reshape(2, 3, 32, 128).transpose(3, 0, 2, 1)),
            "w_in": w_in, "w_br_m": w_br_m, "w_br_a": w_br_a, "w_out": w_out,
            "pbc": pbc, "ppp": pp,
        })
    res = run_bass_kernel_spmd(nc, in_maps, core_ids=list(range(NCORE)))
    R = res.results
    cat = lambda k, ax: np.concatenate([np.asarray(r[k]) for r in R], axis=ax)
    stk = lambda k, ax: np.stack([np.asarray(r[k]) for r in R], axis=ax)
    y_prompt = cat("y_p", 0)
    y_sample = stk("y_s", 0)
    new_k_p = cat("nk_p", 1).reshape(2, B, SEQ, NH_A, 64)
    new_v_p = cat("nv_p", 1).reshape(2, B, SEQ, NH_A, 64)
    new_logf_p = cat("nlf_p", 1)
    new_ssm_p = cat("nssm_p", 1).reshape(2, B, 32, 64, 128)
    ncp = cat("nconv_p", 1)
    new_conv_p = np.ascontiguousarray(ncp.transpose(0, 1, 4, 3, 2)).reshape(2, B, 3, 4096)
    new_k_s = stk("nk_s", 1).reshape(2, NCORE, DEC, NH_A, 64)
    new_v_s = stk("nv_s", 1).reshape(2, NCORE, DEC, NH_A, 64)
    new_logf_s = stk("nlf_s", 1)
    new_ssm_s = stk("nssm_s", 1).reshape(2, NCORE, 32, 64, 128)
    ncs = stk("nconv_s", 1)
    new_conv_s = np.ascontiguousarray(ncs.transpose(0, 1, 4, 3, 2)).reshape(2, NCORE, 3, 4096)
    return (y_prompt.astype(np.float32), y_sample.astype(np.float32), new_k_p, new_v_p, new_logf_p, new_ssm_p, new_conv_p,
            new_k_s, new_v_s, new_logf_s, new_ssm_s, new_conv_s)
```

```python
import numpy as np
import concourse.bass as bass
import concourse.mybir as mybir
from concourse.bass_utils import run_bass_kernel_spmd

F32 = mybir.dt.float32
BF16 = mybir.dt.bfloat16
AF = mybir.ActivationFunctionType
ALU = mybir.AluOpType

ENGS = ("pe", "act", "dve", "pool", "sp")

D = 1024
DIN = 12336
NH_M = 32
NH_A = 16
C_Z, C_X, C_B, C_C, C_DT, C_Q, C_K, C_V, C_F, C_G, C_MM, C_MA = 0, 2048, 4096, 5120, 6144, 6176, 7200, 8224, 9248, 9264, 10288, 11312
DN_ALPHA = (2 * 2) ** 0.25
NPB_L = 2160
NPP_L = 192


class Buf:
    __slots__ = ("name", "w", "r", "dw", "dr", "wsem", "rsem", "wcnt", "rcnt", "psum")

    def __init__(self, name):
        self.name = name
        self.psum = False
        self.w = {}
        self.r = {}
        self.dw = {}
        self.dr = {}
        self.wsem = {}
        self.rsem = {}
        self.wcnt = {}
        self.rcnt = {}


class DBuf:
    def __init__(self, name):
        self.name = name
        self.w = {}


class Tile:
    def __init__(self, t, buf):
        self.t = t
        self.buf = buf

    def __getitem__(self, k):
        return self.t[k]


class Rot:
    def __init__(self, items):
        self.items = items
        self.i = 0

    def next(self):
        x = self.items[self.i % len(self.items)]
        self.i += 1
        return x


class Sched:
    def __init__(self, nc):
        self.nc = nc
        self.ops = {e: [] for e in ENGS}
        self.out_waits = []
        self.nsem = 0
        self.phase = "pro"

    def _sem(self, name):
        self.nsem += 1
        return self.nc.alloc_semaphore(name=f"s{self.nsem}_{name}")

    def sbuf(self, name, shape, dtype):
        return Tile(self.nc.alloc_sbuf_tensor(name, list(shape), dtype), Buf(name))

    def psum(self, name, shape, dtype=F32):
        t = Tile(self.nc.alloc_psum_tensor(name, list(shape), dtype), Buf(name))
        t.buf.psum = True
        return t

    @staticmethod
    def _bufs(xs):
        out = []
        for x in xs:
            if x is None:
                continue
            out.append(x.buf if isinstance(x, Tile) else x)
        return out

    def op(self, eng, fn, reads=(), writes=()):
        reads = self._bufs(reads)
        writes = self._bufs(writes)
        deps = []
        for b in reads:
            for e, i in b.w.items():
                if e == eng and eng == "pe":
                    continue
                deps.append(("e", e, i))
            for sc in b.dw.values():
                deps.append(("d",) + sc)
            if b.psum:
                for e, i in b.r.items():
                    if e != eng:
                        deps.append(("e", e, i))
        for b in writes:
            for e, i in b.w.items():
                if e != eng or eng != "pe":
                    deps.append(("e", e, i))
            for e, i in b.r.items():
                if e != eng or eng != "pe":
                    deps.append(("e", e, i))
            for sc in b.dw.values():
                deps.append(("d",) + sc)
            for sc in b.dr.values():
                deps.append(("d",) + sc)
        idx = len(self.ops[eng])
        self.ops[eng].append({"fn": fn, "deps": deps, "signal": False, "dma": None, "phase": self.phase})
        for b in reads:
            b.r[eng] = idx
        for b in writes:
            b.w[eng] = idx
        return idx

    def dma(self, q, out_ap, in_ap, reads=(), writes=(), final=False, dram_r=None, dram_w=None, serialize=False, **kw):
        reads = self._bufs(reads)
        writes = self._bufs(writes)
        assert len(reads) + len(writes) == 1
        deps = []
        sw = (q == "pool")
        for b in reads:
            for e, i in b.w.items():
                deps.append(("e", e, i))
            for sc in b.dw.values():
                deps.append(("d",) + sc)
        for b in writes:
            for e, i in b.w.items():
                deps.append(("e", e, i))
            for e, i in b.r.items():
                deps.append(("e", e, i))
            for sc in b.dr.values():
                deps.append(("d",) + sc)
            if serialize:
                for sc in b.dw.values():
                    deps.append(("d",) + sc)
        if dram_r is not None:
            for s_, c_ in dram_r.w.values():
                deps.append(("d", s_, c_))
        sem = None
        for b in writes:
            if sw not in b.wsem:
                b.wsem[sw] = self._sem(("ws" if sw else "wh") + b.name)
                b.wcnt[sw] = 0
            b.wcnt[sw] += 16
            sem, cnt = b.wsem[sw], b.wcnt[sw]
            b.dw[id(sem)] = (sem, cnt)
        for b in reads:
            if sw not in b.rsem:
                b.rsem[sw] = self._sem(("rs" if sw else "rh") + b.name)
                b.rcnt[sw] = 0
            b.rcnt[sw] += 16
            sem, cnt = b.rsem[sw], b.rcnt[sw]
            b.dr[id(sem)] = (sem, cnt)
            if final:
                self.out_waits.append(b)
        if dram_w is not None:
            dram_w.w[id(sem)] = (sem, cnt)

        def fn(eng, out_ap=out_ap, in_ap=in_ap, kw=kw):
            return eng.dma_start(out=out_ap, in_=in_ap, **kw)
        self.ops[q].append({"fn": fn, "deps": deps, "signal": False, "dma": sem})

    def emit(self):
        nc = self.nc
        for e in ENGS:
            for o in self.ops[e]:
                for d in o["deps"]:
                    if d[0] == "e":
                        self.ops[d[1]][d[2]]["signal"] = True
        esem = {}
        for e in ENGS:
            c = 0
            for o in self.ops[e]:
                if o["signal"]:
                    assert o["dma"] is None
                    c += 1
                    o["sigval"] = c
            if c:
                esem[e] = self._sem("eng_" + e)
        out_waits = self.out_waits

        def body(e):
            def f(eng):
                known = {}
                for o in self.ops[e]:
                    need = {}
                    for d in o["deps"]:
                        if d[0] == "e":
                            s = esem[d[1]]
                            v = self.ops[d[1]][d[2]]["sigval"]
                        else:
                            s, v = d[1], d[2]
                        k = id(s)
                        if known.get(k, 0) >= v:
                            continue
                        if k not in need or need[k][1] < v:
                            need[k] = (s, v)
                    for k, (s, v) in need.items():
                        eng.wait_ge(s, v)
                        known[k] = v
                    inst = o["fn"](eng)
                    if o["dma"] is not None:
                        inst.then_inc(o["dma"], 16)
                    elif o["signal"]:
                        inst.then_inc(esem[e], 1)
                if e == "pool":
                    seen = set()
                    for b in out_waits:
                        if id(b) in seen:
                            continue
                        seen.add(id(b))
                        for sem_, cnt_ in b.dr.values():
                            eng.wait_ge(sem_, cnt_)
            return f

        with nc.Block() as block:
            block.tensor(body("pe"))
            block.scalar(body("act"))
            block.vector(body("dve"))
            block.gpsimd(body("pool"))
            block.sync(body("sp"))


def build(SEQ, PAST, NB, DEC=16):
    NCHP = SEQ // 128
    NCHS = PAST // 128
    NCH = max(NCHP, NCHS + 1)
    nc = bass.Bass("TRN2", target_bir_lowering=False)

    def din(name, shape, dt=F32):
        return nc.dram_tensor(name, list(shape), dt, kind="ExternalInput").ap()

    def dout(name, shape, dt=F32):
        return nc.dram_tensor(name, list(shape), dt, kind="ExternalOutput").ap()

    def dint(name, shape, dt=BF16):
        return nc.dram_tensor(name, list(shape), dt, kind="Internal").ap()

    xp = din("xp", [NB, SEQ, D])
    xs = din("xs", [DEC, D])
    ck = din("ck", [2, PAST, D])
    cv = din("cv", [2, PAST, D])
    clf = din("clf", [2, PAST, NH_A])
    sssm = din("sssm", [2, 2048, 128])
    sconv = din("sconv", [128, 2, 32, 3])
    w_in = din("w_in", [2, D, DIN])
    w_br_m = din("w_br_m", [2, 2048, D])
    w_br_a = din("w_br_a", [2, D, D])
    w_out = din("w_out", [2, D, D])
    pbc_d = din("pbc", [128, 2 * NPB_L])
    ppp_d = din("ppp", [128, 2 * NPP_L])

    y_p = dout("y_p", [NB, SEQ, D])
    y_s = dout("y_s", [DEC, D])
    nk_p = dout("nk_p", [2, NB, SEQ, D])
    nv_p = dout("nv_p", [2, NB, SEQ, D])
    nlf_p = dout("nlf_p", [2, NB, SEQ, NH_A])
    nssm_p = dout("nssm_p", [2, NB, 2048, 128])
    nconv_p = dout("nconv_p", [2, NB, 128, 32, 3])
    nk_s = dout("nk_s", [2, DEC, D])
    nv_s = dout("nv_s", [2, DEC, D])
    nlf_s = dout("nlf_s", [2, DEC, NH_A])
    nssm_s = dout("nssm_s", [2, 2048, 128])
    nconv_s = dout("nconv_s", [2, 128, 32, 3])

    wb_in = dint("wb_in", [2, D, DIN])
    wb_m = dint("wb_m", [2, 2048, D])
    wb_a = dint("wb_a", [2, D, D])
    wb_o = dint("wb_o", [2, D, D])
    ktd = dint("ktd", [2, NCHP, 128, 8, 128])
    vd = dint("vd", [2, NCHP, 128, D])

    S = Sched(nc)
    import os as _os
    _KSTOP = int(_os.environ.get("KSTOP", "0"))
    DW = DBuf("weights")
    DKT = [DBuf("ktd0"), DBuf("ktd1")]
    DV = [DBuf("vd0"), DBuf("vd1")]

    ident = S.sbuf("ident", [128, 128], F32)
    tri_f = S.sbuf("tri_f", [128, 128], F32)
    gt_f = S.sbuf("gt_f", [128, 128], F32)
    ones_f = S.sbuf("ones_f", [128, 128], F32)
    maskneg = S.sbuf("maskneg", [128, 128], F32)
    tri_b = S.sbuf("tri_b", [128, 128], BF16)
    ones_b = S.sbuf("ones_b", [128, 64], BF16)
    selh = S.sbuf("selh", [80, 16, 128], BF16)
    pbc = S.sbuf("pbc_sb", [128, 2 * NPB_L], F32)
    ppp = S.sbuf("ppp_sb", [128, 2 * NPP_L], F32)
    A_bc = S.sbuf("A_bc", [128, 2, 32], F32)

    xres = [S.sbuf(f"xres{i}", [128, D], F32) for i in range(2)]
    xT = S.sbuf("xT", [128, 8, 128], BF16)
    wbufs = Rot([S.sbuf(f"wbuf{i}", [128, 4096], BF16) for i in range(4)])
    qT = S.sbuf("qT", [128, 8, 128], BF16)
    sgT = S.sbuf("sgT", [128, 8, 128], F32)
    yaT = S.sbuf("yaT", [128, 8, 128], BF16)
    ptr = Rot([S.sbuf(f"pt{i}", [128, 128], BF16) for i in range(6)])
    kvst = Rot([S.sbuf(f"kvst{i}", [128, D], F32) for i in range(2)])
    ktcur = Rot([S.sbuf(f"ktcur{i}", [128, 8, 128], BF16) for i in range(2)])
    vcur = Rot([S.sbuf(f"vcur{i}", [128, D], BF16) for i in range(2)])
    ktl = Rot([S.sbuf(f"ktl{i}", [128, 8, 128], BF16) for i in range(3)])
    vl = Rot([S.sbuf(f"vl{i}", [128, D], BF16) for i in range(3)])
    negc = [S.sbuf(f"negc{l}", [128, NCH, 16], F32) for l in range(2)]
    ccar = [S.sbuf(f"ccar{l}", [128, 16], F32) for l in range(2)]
    cTb = S.sbuf("cTb", [80, 128], BF16)
    cwide = S.sbuf("cwide", [128, 80], F32)
    sm = S.sbuf("sm", [128, 16, 32], F32)
    ST = [S.sbuf(f"ST{l}", [128, 2048], F32) for l in range(2)]
    STb = S.sbuf("STb", [128, 2048], BF16)
    cprev = [S.sbuf(f"cprev{l}", [128, 32, 3], F32) for l in range(2)]
    xpad = Rot([S.sbuf(f"xpad{i}", [128, 132], F32) for i in range(3)])
    cva = Rot([S.sbuf(f"cva{i}", [128, 128], F32) for i in range(3)])
    cvb = Rot([S.sbuf(f"cvb{i}", [128, 128], F32) for i in range(4)])
    xdt = S.sbuf("xdt", [128, 2048], BF16)
    xw = S.sbuf("xw", [128, 2048], BF16)
    xD = S.sbuf("xD", [128, 2048], F32)
    ybuf = S.sbuf("ybuf", [128, 2048], F32)
    BT = S.sbuf("BT", [128, 8, 128], BF16)
    CT = S.sbuf("CT", [128, 8, 128], BF16)
    Btok = S.sbuf("Btok", [128, 8, 128], BF16)
    cbm = S.sbuf("cbm", [128, 8, 128], F32)
    xsB = cbm
    rhsall = Rot([S.sbuf(f"rhsall{i}", [128, 4, 128], F32) for i in range(3)])
    eseg = Rot([S.sbuf(f"eseg{i}", [128, 4, 128], F32) for i in range(3)])
    MTt = Rot([S.sbuf(f"MT{i}", [128, 4, 128], BF16) for i in range(4)])
    t256 = Rot([S.sbuf(f"t256_{i}", [128, 256], F32) for i in range(2)])
    t512 = Rot([S.sbuf(f"t512_{i}", [128, 512], F32) for i in range(2)])
    ymT = S.sbuf("ymT", [128, 16, 128], BF16)
    gaT = S.sbuf("gaT", [128, 8, 128], F32)
    gmT = S.sbuf("gmT", [128, 8, 128], F32)
    uTf = sgT
    uT = S.sbuf("uT", [128, 8, 128], BF16)
    pre = ybuf
    clft = S.sbuf("clft", [128, max(NCHS, 1), 16], F32)
    castb = Buf("castb")

    banks = [S.psum(f"bank{i}", [128, 512], F32) for i in range(8)]
    psA = Rot(banks[0:4])
    ps6 = Rot(banks[0:6])
    psAll = Rot(banks)
    ybk = Rot(banks[4:8])
    oTb = banks[4:6]
    rsb = banks[6:8]

    def PB(l, off, n):
        return pbc.t[:, l * NPB_L + off: l * NPB_L + off + n]

    def PP(l, off, n=1):
        return ppp.t[:, l * NPP_L + off: l * NPP_L + off + n]

    def mm(out, lhsT, rhs, start, stop, reads, writes, **kw):
        S.op("pe", lambda e: e.matmul(out, lhsT=lhsT, rhs=rhs, start=start, stop=stop, **kw), reads=reads, writes=writes)

    def tp(out, in_, n_in_part, reads, writes):
        S.op("pe", lambda e: e.transpose(out=out, in_=in_, identity=ident.t[0:n_in_part, 0:n_in_part]), reads=list(reads) + [ident], writes=writes)

    def act(out, in_, func, reads, writes, **kw):
        S.op("act", lambda e: e.activation(out=out, in_=in_, func=func, **kw), reads=reads, writes=writes)

    def tt(eng, out, in0, in1, op, reads, writes):
        S.op(eng, lambda e: e.tensor_tensor(out=out, in0=in0, in1=in1, op=op), reads=reads, writes=writes)

    def ts(eng, out, in0, s1, s2, op0, op1, reads, writes):
        if op1 is None:
            S.op(eng, lambda e: e.tensor_scalar(out=out, in0=in0, scalar1=s1, scalar2=None, op0=op0), reads=reads, writes=writes)
        else:
            S.op(eng, lambda e: e.tensor_scalar(out=out, in0=in0, scalar1=s1, scalar2=s2, op0=op0, op1=op1), reads=reads, writes=writes)

    def cp(eng, out, in_, reads, writes):
        if eng == "act":
            S.op(eng, lambda e: e.activation(out=out, in_=in_, func=AF.Copy), reads=reads, writes=writes)
        else:
            S.op(eng, lambda e: e.tensor_copy(out=out, in_=in_), reads=reads, writes=writes)

    def ms(eng, ap, val, writes):
        S.op(eng, lambda e: e.memset(ap, val), writes=writes)

    ms("pool", ident.t[:, :], 0.0, [ident])
    S.op("pool", lambda e: e.affine_select(out=ident.t[:, :], in_=ident.t[:, :], pattern=[[-1, 128]], compare_op=ALU.not_equal,
                                           fill=1.0, base=0, channel_multiplier=1), reads=[ident], writes=[ident])
    ms("pool", tri_f.t[:, :], 1.0, [tri_f])
    S.op("pool", lambda e: e.affine_select(out=tri_f.t[:, :], in_=tri_f.t[:, :], pattern=[[1, 128]], compare_op=ALU.is_ge,
                                           fill=0.0, base=0, channel_multiplier=-1), reads=[tri_f], writes=[tri_f])
    ms("pool", gt_f.t[:, :], 1.0, [gt_f])
    S.op("pool", lambda e: e.affine_select(out=gt_f.t[:, :], in_=gt_f.t[:, :], pattern=[[-1, 128]], compare_op=ALU.is_gt,
                                           fill=0.0, base=0, channel_multiplier=1), reads=[gt_f], writes=[gt_f])
    ms("pool", ones_f.t[:, :], 1.0, [ones_f])
    ms("pool", maskneg.t[:, :], 0.0, [maskneg])
    S.op("pool", lambda e: e.affine_select(out=maskneg.t[:, :], in_=maskneg.t[:, :], pattern=[[1, 128]], compare_op=ALU.is_ge,
                                           fill=-30000.0, base=0, channel_multiplier=-1), reads=[maskneg], writes=[maskneg])
    ms("pool", ones_b.t[:, :], 1.0, [ones_b])
    cp("pool", tri_b.t[:, :], tri_f.t[:, :], [tri_f], [tri_b])
    ms("pool", selh.t[:, :, :], 0.0, [selh])
    for p0 in (0, 64):
        S.op("pool", lambda e, p0=p0: e.affine_select(out=selh.t[p0:p0 + 16, :, :], in_=selh.t[p0:p0 + 16, :, :], pattern=[[1, 16], [0, 128]], compare_op=ALU.not_equal,
                                                      fill=1.0, base=0, channel_multiplier=-1), reads=[selh], writes=[selh])
    ms("pool", cwide.t[:, :], 0.0, [cwide])
    S.dma("sp", pbc.t[:, :], pbc_d, writes=[pbc])
    S.dma("sp", ppp.t[:, :], ppp_d, writes=[ppp])
    for l in range(2):
        act(A_bc.t[:, l, :], PB(l, 32, 32), AF.Exp, [pbc], [A_bc])
    ts("dve", A_bc.t[:, :, :], A_bc.t[:, :, :], -1.0, None, ALU.mult, None, [A_bc], [A_bc])

    for l in range(2):
        for r0 in range(0, D, 256):
            S.dma("pool", wb_in[l, r0:r0 + 256, :].rearrange("r (a b) -> r a b", b=1542),
                  w_in[l, r0:r0 + 256, :].rearrange("r (a b) -> r a b", b=1542), writes=[castb], dram_w=DW, serialize=True)
        for r0 in range(0, 2048, 1024):
            S.dma("pool", wb_m[l, r0:r0 + 1024, :], w_br_m[l, r0:r0 + 1024, :], writes=[castb], dram_w=DW, serialize=True)
        S.dma("pool", wb_a[l, :, :], w_br_a[l, :, :], writes=[castb], dram_w=DW, serialize=True)
        S.dma("pool", wb_o[l, :, :], w_out[l, :, :], writes=[castb], dram_w=DW, serialize=True)

    def wload(src2d, nk, c0, ncols):
        wt = wbufs.next()
        view = wt.t[:, 0:nk * ncols].rearrange("p (k n) -> p k n", n=ncols)
        S.dma("sp", view, src2d[:, c0:c0 + ncols].rearrange("(k p) n -> p k n", p=128), writes=[wt], dram_r=DW)
        return wt, view

    class _Stop(Exception):
        pass

    def stage(n):
        S.phase = f"st{n}"
        if _KSTOP == n:
            raise _Stop()

    def tile_layer(l, TS, xr, chunk_idx, past_chunks, out_k, out_v, out_lf):
        Win = wb_in[l]
        for g in range(2):
            bk = psAll.next()
            for c in range(4):
                tp(bk.t[:, c * 128:c * 128 + TS], xr.t[0:TS, (4 * g + c) * 128:(4 * g + c + 1) * 128], TS, [xr], [bk])
            act(xT.t[:, 4 * g:4 * g + 4, 0:TS], bk.t[:, :].rearrange("p (c t) -> p c t", t=128)[:, :, 0:TS], AF.Copy, [bk], [xT])

        def proj_fm(src2d, c0, nchunk, nk, rhsT, evac):
            done = 0
            while done < nchunk:
                n = min(4, nchunk - done)
                wt, wv = wload(src2d, nk, c0 + done * 128, n * 128)
                for jj in range(n):
                    bk = psAll.next()
                    for kc in range(nk):
                        mm(bk.t[:, 0:TS], wv[:, kc, jj * 128:(jj + 1) * 128], rhsT.t[:, kc, 0:TS], kc == 0, kc == nk - 1, [wt, rhsT], [bk])
                    evac(done + jj, bk)
                done += n

        def proj_tm(src2d, c0, ncols, nk, lhsT, evac, wpre=None):
            if wpre is None:
                wt, wv = wload(src2d, nk, c0, ncols)
            else:
                wt, wv = wpre
            bk = psAll.next()
            for kc in range(nk):
                mm(bk.t[0:TS, 0:ncols], lhsT.t[:, kc, 0:TS], wv[:, kc, 0:ncols], kc == 0, kc == nk - 1, [wt, lhsT], [bk])
            evac(bk)

        stage(10)
        proj_fm(Win, C_Q, 8, 8, xT, lambda j, bk: act(qT.t[:, j, 0:TS], bk.t[:, 0:TS], AF.Copy, [bk], [qT], scale=0.125))
        ktc = ktcur.next()
        vc = vcur.next()
        kst_t = kvst.next()
        for blk in range(2):
            proj_tm(Win, C_K + blk * 512, 512, 8, xT, lambda bk, blk=blk: cp("dve", kst_t.t[0:TS, blk * 512:(blk + 1) * 512], bk.t[0:TS, 0:512], [bk], [kst_t]))
        stage(11)
        S.dma("pool", out_k, kst_t.t[0:TS, :], reads=[kst_t], final=True)
        stage(12)
        vst_t = kvst.next()
        for blk in range(2):
            def ev(bk, blk=blk):
                cp("dve", vst_t.t[0:TS, blk * 512:(blk + 1) * 512], bk.t[0:TS, 0:512], [bk], [vst_t])
                act(vc.t[0:TS, blk * 512:(blk + 1) * 512], bk.t[0:TS, 0:512], AF.Copy, [bk], [vc])
            proj_tm(Win, C_V + blk * 512, 512, 8, xT, ev)
        S.dma("pool", out_v, vst_t.t[0:TS, :], reads=[vst_t], final=True)
        for g2 in range(2):
            bk = psAll.next()
            for cc in range(4):
                j = g2 * 4 + cc
                tp(bk.t[:, cc * 128:cc * 128 + TS], kst_t.t[0:TS, j * 128:(j + 1) * 128], TS, [kst_t], [bk])
            act(ktc.t[:, g2 * 4:g2 * 4 + 4, 0:TS], bk.t[:, :].rearrange("p (c t) -> p c t", t=128)[:, :, 0:TS], AF.Copy, [bk], [ktc])
        if past_chunks is not None and chunk_idx is not None and TS == 128 and chunk_idx < NCHP - 1:
            S.dma("pool", ktd[l, chunk_idx], ktc.t[:, :, :], reads=[ktc], dram_w=DKT[l], final=True)
            S.dma("pool", vd[l, chunk_idx], vc.t[:, :], reads=[vc], dram_w=DV[l], final=True)
        stage(13)
        ci = chunk_idx
        LSP, LF, CC = sm.t[:, 0, 0:16], sm.t[:, 1, 0:16], sm.t[:, 2, 0:16]

        def ev_f(bk):
            tt("dve", sm.t[0:TS, 0, 0:16], bk.t[0:TS, 0:16], PB(l, 96, 16)[0:TS, :], ALU.add, [bk, pbc], [sm])
            act(sm.t[0:TS, 0, 0:16], sm.t[0:TS, 0, 0:16], AF.Exp, [sm], [sm], scale=-1.0)
            act(sm.t[0:TS, 0, 0:16], sm.t[0:TS, 0, 0:16], AF.Ln, [sm], [sm], bias=1.0, scale=1.0)
            ts("dve", sm.t[0:TS, 1, 0:16], sm.t[0:TS, 0, 0:16], -1.0, None, ALU.mult, None, [sm], [sm])
        proj_tm(Win, C_F, 16, 8, xT, ev_f)
        proj_fm(Win, C_G, 8, 8, xT, lambda j, bk: act(sgT.t[:, j, 0:TS], bk.t[:, 0:TS], AF.Silu, [bk], [sgT]))
        S.dma("pool", out_lf, sm.t[0:TS, 1, 0:16], reads=[sm], final=True)
        cumsum_chunk(l, TS, sm.t[0:TS, 1, 0:16], sm, ci)
        ts("dve", cwide.t[0:TS, 0:16], negc[l].t[0:TS, ci, :], -1.0, None, ALU.mult, None, [negc[l]], [cwide])
        ts("dve", cwide.t[0:TS, 64:80], negc[l].t[0:TS, ci, :], -1.0, None, ALU.mult, None, [negc[l]], [cwide])
        bk = psAll.next()
        tp(bk.t[0:80, 0:TS], cwide.t[0:TS, 0:80], TS, [cwide], [bk])
        act(cTb.t[0:80, 0:TS], bk.t[0:80, 0:TS], AF.Copy, [bk], [cTb])
        stage(14)
        stage(15)
        chunks = []
        for pc_i, pc in enumerate(past_chunks):
            chunks.append((pc, pc_i, 128, False))
        chunks.append((lambda: (ktc, vc), ci, TS, True))
        for half in range(2):
          ob_h = banks[6]
          rb_h = banks[7]
          for b_ in (ob_h, rb_h):
            ms("dve", b_.t[:, :], 0.0, [b_])
          items = [(cix, j) for cix in range(len(chunks)) for j in range(half * 4, half * 4 + 4)]
          loaded = {}

          def stA(it):
              cix, j = it
              getter, cidx, nk, diag = chunks[cix]
              if cix not in loaded:
                  loaded[cix] = getter()
              kt_t, v_t = loaded[cix]
              sbs = [ps6.next(), ps6.next()]
              for e_ in range(2):
                  hb = 64 * e_
                  mm(sbs[e_].t[0:nk, 0:TS], kt_t.t[hb:hb + 64, j, 0:nk], qT.t[hb:hb + 64, j, 0:TS], True, False, [kt_t, qT], [sbs[e_]])
              for e_ in range(2):
                  hb = 64 * e_
                  h = 2 * j + e_
                  mm(sbs[e_].t[0:nk, 0:TS], selh.t[hb:hb + 16, h, 0:nk], cTb.t[hb:hb + 16, 0:TS], False, True, [selh, cTb], [sbs[e_]])
              pts_ = []
              for e_ in range(2):
                  h = 2 * j + e_
                  sb = sbs[e_]
                  pt = ptr.next()
                  if diag:
                      tm_ = cva.next()
                      S.op("dve", lambda e, tm_=tm_, sb=sb, h=h, cidx=cidx, nk=nk: e.scalar_tensor_tensor(
                          out=tm_.t[0:nk, 0:TS], in0=sb.t[0:nk, 0:TS], scalar=negc[l].t[0:nk, cidx, h:h + 1], in1=maskneg.t[0:nk, 0:TS],
                          op0=ALU.add, op1=ALU.add), reads=[sb, negc[l], maskneg], writes=[tm_])
                      act(pt.t[0:nk, 0:TS], tm_.t[0:nk, 0:TS], AF.Exp, [tm_], [pt])
                  else:
                      act(pt.t[0:nk, 0:TS], sb.t[0:nk, 0:TS], AF.Exp, [sb, negc[l]], [pt], bias=negc[l].t[0:nk, cidx, h:h + 1], scale=1.0)
                  pts_.append(pt)
              return pts_

          def stB(it, pts_):
              cix, j = it
              getter, cidx, nk, diag = chunks[cix]
              kt_t, v_t = loaded[cix]
              ob = ob_h
              rb = rb_h
              col = (j % 4) * 128
              for e_ in range(2):
                  hb = 64 * e_
                  h = 2 * j + e_
                  mm(ob.t[hb:hb + 64, col:col + TS], v_t.t[0:nk, h * 64:(h + 1) * 64], pts_[e_].t[0:nk, 0:TS], False, False, [v_t, pts_[e_]], [ob], skip_group_check=True)
              for e_ in range(2):
                  hb = 64 * e_
                  mm(rb.t[hb:hb + 64, col:col + TS], ones_b.t[0:nk, 0:64], pts_[e_].t[0:nk, 0:TS], False, False, [ones_b, pts_[e_]], [rb], skip_group_check=True)

          LOOK = 2
          pts = {}
          for i in range(len(items) + LOOK):
              if i < len(items):
                  pts[i] = stA(items[i])
              if i - LOOK >= 0:
                  stB(items[i - LOOK], pts.pop(i - LOOK))
          tmp = t512.next()
          tv = tmp.t[:, :].rearrange("p (c t) -> p c t", t=128)[:, :, 0:TS]
          S.op("dve", lambda e, tv=tv, rb_h=rb_h: e.reciprocal(out=tv, in_=rb_h.t[:, :].rearrange("p (c t) -> p c t", t=128)[:, :, 0:TS]), reads=[rb_h], writes=[tmp])
          tt("dve", tv, ob_h.t[:, :].rearrange("p (c t) -> p c t", t=128)[:, :, 0:TS], tv, ALU.mult, [ob_h, tmp], [tmp])
          tt("pool", yaT.t[:, 4 * half:4 * half + 4, 0:TS], tv, sgT.t[:, 4 * half:4 * half + 4, 0:TS], ALU.mult, [tmp, sgT], [yaT])

        stage(16)
        DT, DTA, ACUM, EA, WEND, EEND, TMPD = (sm.t[:, 3, :], sm.t[:, 4, :], sm.t[:, 5, :], sm.t[:, 6, :], sm.t[:, 7, :], sm.t[:, 8, :], sm.t[:, 9, :])

        def ev_dt(bk):
            tt("dve", DT[0:TS], bk.t[0:TS, 0:32], PB(l, 0, 32)[0:TS, :], ALU.add, [bk, pbc], [sm])
            act(DT[0:TS], DT[0:TS], AF.Exp, [sm], [sm])
            act(DT[0:TS], DT[0:TS], AF.Ln, [sm], [sm], bias=1.0, scale=1.0)
            tt("dve", DTA[0:TS], DT[0:TS], A_bc.t[0:TS, l, :], ALU.mult, [sm, A_bc], [sm])
        proj_tm(Win, C_DT, 32, 8, xT, ev_dt)
        cp("act", STb.t[:, :], ST[l].t[:, :], [ST[l]], [STb])

        stage(17)

        pend = []

        def conv_y(c, tb):
            if c < 16:
                act(ybuf.t[:, c * 128:c * 128 + TS], tb.t[:, 0:TS], AF.Silu, [tb], [ybuf])
            elif c < 24:
                g = c - 16
                act(xsB.t[:, g, 0:TS], tb.t[:, 0:TS], AF.Silu, [tb], [xsB])
                cp("pool", BT.t[:, g, 0:TS], xsB.t[:, g, 0:TS], [xsB], [BT])
            else:
                g = c - 24
                act(CT.t[:, g, 0:TS], tb.t[:, 0:TS], AF.Silu, [tb], [CT])

        nxt_xp = [None]

        def ev_conv(c, bk):
            if nxt_xp[0] is None:
                xp_ = xpad.next()
                cp("pool", xp_.t[:, 0:3], cprev[l].t[:, c, :], [cprev[l]], [xp_])
            else:
                xp_ = nxt_xp[0]
            ta = cva.next()
            tb = cvb.next()
            act(xp_.t[:, 3:3 + TS], bk.t[:, 0:TS], AF.Copy, [bk], [xp_])
            act(ta.t[:, 0:TS], bk.t[:, 0:TS], AF.Identity, [bk, ppp], [ta], scale=PP(l, c * 4 + 3), bias=PP(l, 128 + c))
            if c + 1 < 32:
                nxp = xpad.next()
                cp("pool", nxp.t[:, 0:3], cprev[l].t[:, c + 1, :], [cprev[l]], [nxp])
                nxt_xp[0] = nxp
            else:
                nxt_xp[0] = None
            S.op("dve", lambda e: e.scalar_tensor_tensor(out=tb.t[:, 0:TS], in0=xp_.t[:, 0:TS], scalar=PP(l, c * 4 + 0), in1=ta.t[:, 0:TS], op0=ALU.mult, op1=ALU.add), reads=[xp_, ta, ppp], writes=[tb])
            S.op("dve", lambda e: e.scalar_tensor_tensor(out=ta.t[:, 0:TS], in0=xp_.t[:, 1:1 + TS], scalar=PP(l, c * 4 + 1), in1=tb.t[:, 0:TS], op0=ALU.mult, op1=ALU.add), reads=[xp_, tb, ppp], writes=[ta])
            S.op("dve", lambda e: e.scalar_tensor_tensor(out=tb.t[:, 0:TS], in0=xp_.t[:, 2:2 + TS], scalar=PP(l, c * 4 + 2), in1=ta.t[:, 0:TS], op0=ALU.mult, op1=ALU.add), reads=[xp_, ta, ppp], writes=[tb])
            cp("pool", cprev[l].t[:, c, :], xp_.t[:, TS:TS + 3], [xp_], [cprev[l]])
            pend.append((c, tb))
            if len(pend) > 2:
                conv_y(*pend.pop(0))
        proj_fm(Win, C_X, 32, 8, xT, ev_conv)
        while pend:
            conv_y(*pend.pop(0))
        bk1 = psAll.next()
        mm(bk1.t[0:TS, 0:32], tri_f.t[0:TS, 0:TS], DTA[0:TS], True, True, [tri_f, sm], [bk1])
        bk2 = psAll.next()
        mm(bk2.t[0:128, 0:32], ones_f.t[0:TS, 0:128], DTA[0:TS], True, True, [ones_f, sm], [bk2])
        act(ACUM[0:TS], bk1.t[0:TS, 0:32], AF.Copy, [bk1], [sm])
        act(EA[0:TS], bk1.t[0:TS, 0:32], AF.Exp, [bk1], [sm])
        act(EEND, bk2.t[:, 0:32], AF.Exp, [bk2], [sm])
        tt("dve", TMPD[0:TS], bk2.t[0:TS, 0:32], ACUM[0:TS], ALU.subtract, [bk2, sm], [sm])
        act(TMPD[0:TS], TMPD[0:TS], AF.Exp, [sm], [sm])
        tt("dve", WEND[0:TS], TMPD[0:TS], DT[0:TS], ALU.mult, [sm], [sm])
        for q4 in range(4):
            xb = psAll.next()
            for cc in range(4):
                c = q4 * 4 + cc
                tp(xb.t[0:TS, cc * 128:(cc + 1) * 128], ybuf.t[:, c * 128:c * 128 + TS], 128, [ybuf], [xb])
            c0 = q4 * 512
            h0 = c0 // 64
            xv = xb.t[0:TS, :].rearrange("p (h d) -> p h d", d=64)
            tt("dve", xdt.t[0:TS, c0:c0 + 512].rearrange("p (h d) -> p h d", d=64), xv, DT[0:TS, h0:h0 + 8].unsqueeze(2).broadcast_to([TS, 8, 64]), ALU.mult, [xb, sm], [xdt])
            tt("dve", xw.t[0:TS, c0:c0 + 512].rearrange("p (h d) -> p h d", d=64), xv, WEND[0:TS, h0:h0 + 8].unsqueeze(2).broadcast_to([TS, 8, 64]), ALU.mult, [xb, sm], [xw])
            tt("dve", xD.t[0:TS, c0:c0 + 512].rearrange("p (h d) -> p h d", d=64), xv, PB(l, 64, 32)[0:TS, h0:h0 + 8].unsqueeze(2).broadcast_to([TS, 8, 64]), ALU.mult, [xb, pbc], [xD])
        for q4 in range(2):
            xb = psAll.next()
            for cc in range(4):
                g = q4 * 4 + cc
                tp(xb.t[0:TS, cc * 128:(cc + 1) * 128], xsB.t[:, g, 0:TS], 128, [xsB], [xb])
            act(Btok.t[0:TS, q4 * 4:q4 * 4 + 4, :], xb.t[0:TS, :].rearrange("p (g n) -> p g n", n=128), AF.Copy, [xb], [Btok])

        stage(18)
        for half in range(2):
            bk = psAll.next()
            for gg in range(4):
                g = half * 4 + gg
                mm(bk.t[0:TS, gg * 128:gg * 128 + TS], BT.t[:, g, 0:TS], CT.t[:, g, 0:TS], True, True, [BT, CT], [bk])
            tt("dve", cbm.t[0:TS, half * 4:half * 4 + 4, 0:TS], bk.t[0:TS, :].rearrange("p (g t) -> p g t", t=128)[:, :, 0:TS],
               tri_f.t[0:TS, 0:TS].unsqueeze(1).broadcast_to([TS, 4, TS]), ALU.mult, [bk, tri_f], [cbm])
        def s1(g):
            ra = rhsall.next()
            tt("pool", ra.t[0:TS, :, 0:TS], DTA[0:TS, 4 * g:4 * g + 4].unsqueeze(2).broadcast_to([TS, 4, TS]),
               tri_f.t[0:TS, 0:TS].unsqueeze(1).broadcast_to([TS, 4, TS]), ALU.mult, [sm, tri_f], [ra])
            bk = psA.next()
            if TS == 128:
                mm(bk.t[0:TS, 0:512], gt_f.t[0:TS, 0:TS], ra.t[0:TS, :, :].rearrange("p r t -> p (r t)"), True, True, [gt_f, ra], [bk])
            else:
                for r in range(4):
                    mm(bk.t[0:TS, r * 128:r * 128 + TS], gt_f.t[0:TS, 0:TS], ra.t[0:TS, r, 0:TS], True, True, [gt_f, ra], [bk])
            es = eseg.next()
            act(es.t[0:TS, :, 0:TS], bk.t[0:TS, :].rearrange("p (r t) -> p r t", t=128)[:, :, 0:TS], AF.Exp, [bk], [es])
            mt = MTt.next()
            tt("dve", mt.t[0:TS, :, 0:TS], es.t[0:TS, :, 0:TS], cbm.t[0:TS, g, 0:TS].unsqueeze(1).broadcast_to([TS, 4, TS]), ALU.mult, [es, cbm], [mt])
            return mt

        def s2(g, mt):
            yb = ybk.next()
            for r in range(4):
                h = 4 * g + r
                mm(yb.t[0:TS, r * 64:(r + 1) * 64], mt.t[0:TS, r, 0:TS], xdt.t[0:TS, h * 64:(h + 1) * 64], True, True, [mt, xdt], [yb])
            mm(yb.t[0:TS, 256:512], CT.t[:, g, 0:TS], STb.t[:, 256 * g:256 * g + 256], True, True, [CT, STb], [yb])
            t1 = t256.next()
            tt("dve", t1.t[0:TS, :].rearrange("p (r d) -> p r d", d=64), yb.t[0:TS, 256:512].rearrange("p (r d) -> p r d", d=64),
               EA[0:TS, 4 * g:4 * g + 4].unsqueeze(2).broadcast_to([TS, 4, 64]), ALU.mult, [yb, sm], [t1])
            tt("dve", t1.t[0:TS, :], yb.t[0:TS, 0:256], t1.t[0:TS, :], ALU.add, [yb, t1], [t1])
            tt("dve", ybuf.t[0:TS, 256 * g:256 * g + 256], t1.t[0:TS, :], xD.t[0:TS, 256 * g:256 * g + 256], ALU.add, [t1, xD], [ybuf])

        mts = {}
        for g in range(8 + 2):
            if g < 8:
                mts[g] = s1(g)
            if g - 2 >= 0:
                s2(g - 2, mts.pop(g - 2))
        stage(19)
        for gp in range(4):
            bk = psAll.next()
            for gg in range(2):
                g = gp * 2 + gg
                mm(bk.t[:, gg * 256:(gg + 1) * 256], Btok.t[0:TS, g, :], xw.t[0:TS, 256 * g:256 * g + 256], True, True, [Btok, xw], [bk])
            sv = ST[l].t[:, gp * 512:(gp + 1) * 512]
            tt("pool", sv.rearrange("p (h d) -> p h d", d=64), sv.rearrange("p (h d) -> p h d", d=64),
               EEND[:, gp * 8:gp * 8 + 8].unsqueeze(2).broadcast_to([128, 8, 64]), ALU.mult, [ST[l], sm], [ST[l]])
            tt("dve", sv, sv, bk.t[:, :], ALU.add, [ST[l], bk], [ST[l]])
        stage(20)
        for blk in range(4):
            def ev_z(bk, blk=blk):
                tz = t512.next()
                act(tz.t[0:TS, :], bk.t[0:TS, :], AF.Silu, [bk], [tz])
                tt("dve", ybuf.t[0:TS, blk * 512:(blk + 1) * 512], ybuf.t[0:TS, blk * 512:(blk + 1) * 512], tz.t[0:TS, :], ALU.mult, [ybuf, tz], [ybuf])
            proj_tm(Win, C_Z + blk * 512, 512, 8, xT, ev_z)
        SS, RST = sm.t[:, 10, 0:8], sm.t[:, 11, 0:8]
        tt("pool", xD.t[0:TS, :], ybuf.t[0:TS, :], ybuf.t[0:TS, :], ALU.mult, [ybuf], [xD])
        S.op("dve", lambda e: e.tensor_reduce(out=SS[0:TS], in_=xD.t[0:TS, :].rearrange("p (g c) -> p g c", c=256), axis=mybir.AxisListType.X, op=ALU.add), reads=[xD], writes=[sm])
        act(RST[0:TS], SS[0:TS], AF.Sqrt, [sm], [sm], bias=1e-5, scale=1.0 / 256.0)
        S.op("dve", lambda e: e.reciprocal(out=RST[0:TS], in_=RST[0:TS]), reads=[sm], writes=[sm])
        tt("dve", ybuf.t[0:TS, :].rearrange("p (g c) -> p g c", c=256), ybuf.t[0:TS, :].rearrange("p (g c) -> p g c", c=256),
           RST[0:TS].unsqueeze(2).broadcast_to([TS, 8, 256]), ALU.mult, [ybuf, sm], [ybuf])
        proj_fm(Win, C_MA, 8, 8, xT, lambda j, bk: act(gaT.t[:, j, 0:TS], bk.t[:, 0:TS], AF.Sigmoid, [bk, ppp], [gaT], bias=PP(l, 176 + 8 + j), scale=1.0))
        proj_fm(Win, C_MM, 8, 8, xT, lambda j, bk: act(gmT.t[:, j, 0:TS], bk.t[:, 0:TS], AF.Sigmoid, [bk, ppp], [gmT], bias=PP(l, 176 + j), scale=1.0))
        proj_fm(wb_a[l], 0, 8, 8, yaT, lambda j, bk: tt("dve", uTf.t[:, j, 0:TS], bk.t[:, 0:TS], gaT.t[:, j, 0:TS], ALU.mult, [bk, gaT], [uTf]))
        for q4 in range(4):
            bk = psAll.next()
            for cc in range(4):
                c = q4 * 4 + cc
                tp(bk.t[:, cc * 128:cc * 128 + TS], ybuf.t[0:TS, c * 128:(c + 1) * 128], TS, [ybuf], [bk])
            tt("dve", ymT.t[:, q4 * 4:q4 * 4 + 4, 0:TS], bk.t[:, :].rearrange("p (c t) -> p c t", t=128)[:, :, 0:TS],
               PP(l, 160 + q4 * 4, 4).unsqueeze(2).broadcast_to([128, 4, TS]), ALU.mult, [bk, ppp], [ymT])

        stage(21)

        def ev_m(j, bk):
            tmp = cva.next()
            tt("dve", tmp.t[:, 0:TS], bk.t[:, 0:TS], gmT.t[:, j, 0:TS], ALU.mult, [bk, gmT], [tmp])
            tt("pool", uT.t[:, j, 0:TS], tmp.t[:, 0:TS], uTf.t[:, j, 0:TS], ALU.add, [tmp, uTf], [uT])
        for blk in range(4):
            wt, wv = wload(wb_m[l], 16, blk * 256, 256)
            for jj in range(2):
                j = blk * 2 + jj
                bk = psAll.next()
                for kc in range(16):
                    mm(bk.t[:, 0:TS], wv[:, kc, jj * 128:(jj + 1) * 128], ymT.t[:, kc, 0:TS], kc == 0, kc == 15, [wt, ymT], [bk])
                ev_m(j, bk)
        for blk in range(2):
            def ev_o(bk, blk=blk):
                S.op("dve", lambda e: e.scalar_tensor_tensor(out=pre.t[0:TS, blk * 512:(blk + 1) * 512], in0=xr.t[0:TS, blk * 512:(blk + 1) * 512], scalar=float(DN_ALPHA),
                                                             in1=bk.t[0:TS, 0:512], op0=ALU.mult, op1=ALU.add), reads=[xr, bk], writes=[pre])
            proj_tm(wb_o[l], blk * 512, 512, 8, uT, ev_o)
        STATS, MV, RS2 = sm.t[:, 12, 0:12], sm.t[:, 13, 0:2], sm.t[:, 13, 2:3]
        S.op("dve", lambda e: e.bn_stats(out=STATS[0:TS, 0:6], in_=pre.t[0:TS, 0:512]), reads=[pre], writes=[sm])
        S.op("dve", lambda e: e.bn_stats(out=STATS[0:TS, 6:12], in_=pre.t[0:TS, 512:1024]), reads=[pre], writes=[sm])
        S.op("dve", lambda e: e.bn_aggr(out=MV[0:TS], in_=STATS[0:TS]), reads=[sm], writes=[sm])
        act(RS2[0:TS], MV[0:TS, 1:2], AF.Sqrt, [sm], [sm], bias=1e-5, scale=1.0)
        S.op("dve", lambda e: e.reciprocal(out=RS2[0:TS], in_=RS2[0:TS]), reads=[sm], writes=[sm])
        ts("dve", pre.t[0:TS, 0:D], pre.t[0:TS, 0:D], MV[0:TS, 0:1], RS2[0:TS], ALU.subtract, ALU.mult, [pre, sm], [pre])
        tt("pool", pre.t[0:TS, 0:D], pre.t[0:TS, 0:D], PB(l, 112, 1024)[0:TS, :], ALU.mult, [pre, pbc], [pre])
        tt("dve", xr.t[0:TS, :], pre.t[0:TS, 0:D], PB(l, 1136, 1024)[0:TS, :], ALU.add, [pre, pbc], [xr])

    def cumsum_chunk(l, TS, lf_ap, lf_tile, ci):
        bk1 = psAll.next()
        mm(bk1.t[0:TS, 0:16], tri_f.t[0:TS, 0:TS], lf_ap, True, True, [tri_f, lf_tile], [bk1])
        bk2 = psAll.next()
        mm(bk2.t[0:128, 0:16], ones_f.t[0:TS, 0:128], lf_ap, True, True, [ones_f, lf_tile], [bk2])
        tt("dve", negc[l].t[0:TS, ci, :], bk1.t[0:TS, 0:16], ccar[l].t[0:TS, :], ALU.add, [bk1, ccar[l]], [negc[l]])
        ts("dve", negc[l].t[0:TS, ci, :], negc[l].t[0:TS, ci, :], -1.0, None, ALU.mult, None, [negc[l]], [negc[l]])
        tt("dve", ccar[l].t[:, :], ccar[l].t[:, :], bk2.t[:, 0:16], ALU.add, [ccar[l], bk2], [ccar[l]])

    def flush_state(l, TS_unused, out_ssm, out_conv):
        for q4 in range(4):
            bk = psAll.next()
            for cc in range(4):
                c = q4 * 4 + cc
                tp(bk.t[:, cc * 128:(cc + 1) * 128], ST[l].t[:, c * 128:(c + 1) * 128], 128, [ST[l]], [bk])
            act(xD.t[:, q4 * 512:(q4 + 1) * 512].rearrange("p (c n) -> p c n", n=128), bk.t[:, :].rearrange("p (c n) -> p c n", n=128), AF.Copy, [bk], [xD])
        S.dma("pool", out_ssm.rearrange("(c p) n -> p c n", p=128), xD.t[:, :].rearrange("p (c n) -> p c n", n=128), reads=[xD], final=True)
        S.dma("pool", out_conv, cprev[l].t[:, :, :], reads=[cprev[l]], final=True)

    def _main_body():
        xi = 0
        for b in range(NB):
            for l in range(2):
                ms("pool", ST[l].t[:, :], 0.0, [ST[l]])
                ms("pool", cprev[l].t[:, :, :], 0.0, [cprev[l]])
                ms("pool", ccar[l].t[:, :], 0.0, [ccar[l]])
            for ci in range(NCHP):
                xr = xres[xi % 2]
                xi += 1
                S.dma("sp", xr.t[:, :], xp[b, ci * 128:(ci + 1) * 128, :], writes=[xr])
                for l in range(2):
                    def mk(c, l=l):
                        def g():
                            kt_t = ktl.next()
                            v_t = vl.next()
                            S.dma("sp", kt_t.t[:, :, :], ktd[l, c], writes=[kt_t], dram_r=DKT[l])
                            S.dma("sp", v_t.t[:, :], vd[l, c], writes=[v_t], dram_r=DV[l])
                            return kt_t, v_t
                        return g
                    tile_layer(l, 128, xr, ci, [mk(c) for c in range(ci)],
                               nk_p[l, b, ci * 128:(ci + 1) * 128, :], nv_p[l, b, ci * 128:(ci + 1) * 128, :], nlf_p[l, b, ci * 128:(ci + 1) * 128, :])
                    stage(30 + l)
                S.dma("pool", y_p[b, ci * 128:(ci + 1) * 128, :], xr.t[:, :], reads=[xr], final=True)
                stage(2)
            for l in range(2):
                flush_state(l, 128, nssm_p[l, b], nconv_p[l, b])
        stage(3)
        xr = xres[xi % 2]
        S.dma("sp", xr.t[0:DEC, :], xs, writes=[xr])
        S.dma("sp", cprev[0].t[:, :, :], sconv[:, 0], writes=[cprev[0]])
        S.dma("sp", cprev[1].t[:, :, :], sconv[:, 1], writes=[cprev[1]])
        for l in range(2):
            S.dma("sp", xD.t[:, :].rearrange("p (c n) -> p c n", n=128), sssm[l].rearrange("(c p) n -> p c n", p=128), writes=[xD])
            for q4 in range(4):
                bk = psAll.next()
                for cc in range(4):
                    c = q4 * 4 + cc
                    tp(bk.t[:, cc * 128:(cc + 1) * 128], xD.t[:, c * 128:(c + 1) * 128], 128, [xD], [bk])
                act(ST[l].t[:, q4 * 512:(q4 + 1) * 512], bk.t[:, :], AF.Copy, [bk], [ST[l]])
            ms("pool", ccar[l].t[:, :], 0.0, [ccar[l]])
            S.dma("sp", clft.t[:, :, :], clf[l].rearrange("(c p) h -> p c h", p=128), writes=[clft])
            for c in range(NCHS):
                cumsum_chunk(l, 128, clft.t[:, c, :], clft, c)

            def mk(c, l=l):
                def g():
                    ks = kvst.next()
                    S.dma("sp", ks.t[:, :], ck[l, c * 128:(c + 1) * 128, :], writes=[ks])
                    kt_t = ktl.next()
                    for g2 in range(2):
                        bk = psA.next()
                        for cc in range(4):
                            tp(bk.t[:, cc * 128:(cc + 1) * 128], ks.t[:, (g2 * 4 + cc) * 128:(g2 * 4 + cc + 1) * 128], 128, [ks], [bk])
                        act(kt_t.t[:, g2 * 4:g2 * 4 + 4, :], bk.t[:, :].rearrange("p (c t) -> p c t", t=128), AF.Copy, [bk], [kt_t])
                    v_t = vl.next()
                    S.dma("pool", v_t.t[:, :], cv[l, c * 128:(c + 1) * 128, :], writes=[v_t])
                    return kt_t, v_t
                return g
            tile_layer(l, DEC, xr, NCHS, [mk(c) for c in range(NCHS)], nk_s[l], nv_s[l], nlf_s[l])
            flush_state(l, DEC, nssm_s[l], nconv_s[l])
        S.dma("pool", y_s, xr.t[0:DEC, :], reads=[xr], final=True)

    try:
        stage(1)
        _main_body()
    except _Stop:
        pass
    S.emit()
    if _os.environ.get("KDUMP"):
        import json as _json
        _json.dump({e: [o.get("phase", "dma") for o in S.ops[e]] for e in ENGS}, open(_os.environ["KDUMP"], "w"))
    return nc


_CACHE = {}


def _host_params(conv_w, conv_b, dt_bias, a_log, d_skip, mnorm_w, b_f, b_merge, ln_g, ln_b):
    pb = np.zeros((2 * NPB_L,), np.float32)
    pp = np.zeros((128, 2 * NPP_L), np.float32)
    for l in range(2):
        o = l * NPB_L
        pb[o:o + 32] = dt_bias[l]
        pb[o + 32:o + 64] = a_log[l]
        pb[o + 64:o + 96] = d_skip[l]
        pb[o + 96:o + 112] = b_f[l]
        pb[o + 112:o + 1136] = ln_g[l]
        pb[o + 1136:o + 2160] = ln_b[l]
        o = l * NPP_L
        pp[:, o:o + 128] = conv_w[l].reshape(4, 32, 128).transpose(2, 1, 0).reshape(128, 128)
        pp[:, o + 128:o + 160] = conv_b[l].reshape(32, 128).T
        pp[:, o + 160:o + 176] = mnorm_w[l].reshape(16, 128).T
        pp[:, o + 176:o + 192] = b_merge[l].reshape(16, 128).T
    pbc = np.ascontiguousarray(np.broadcast_to(pb[None, :], (128, 2 * NPB_L)))
    return pbc, pp


def kernel(x_prompt, x_sample, cache_k, cache_v, cache_logf, state_ssm, state_conv, w_in, conv_w,
           conv_b, dt_bias, a_log, d_skip, mnorm_w, b_f, b_merge, w_br_m, w_br_a, w_out, ln_g, ln_b):
    f = lambda a: np.ascontiguousarray(np.asarray(a, dtype=np.float32))
    x_prompt, x_sample, cache_k, cache_v, cache_logf, state_ssm, state_conv = map(f, (x_prompt, x_sample, cache_k, cache_v, cache_logf, state_ssm, state_conv))
    w_in, w_br_m, w_br_a, w_out = map(f, (w_in, w_br_m, w_br_a, w_out))
    B, SEQ, _ = x_prompt.shape
    NCORE, DEC, _ = x_sample.shape
    PAST = cache_k.shape[2]
    NB = B // NCORE
    key = (SEQ, PAST, NB, DEC)
    if key not in _CACHE:
        _CACHE[key] = build(SEQ, PAST, NB, DEC)
    nc = _CACHE[key]
    pbc, pp = _host_params(*map(f, (conv_w, conv_b, dt_bias, a_log, d_skip, mnorm_w, b_f, b_merge, ln_g, ln_b)))
    in_maps = []
    for c in range(NCORE):
        in_maps.append({
            "xp": x_prompt[c * NB:(c + 1) * NB],
            "xs": x_sample[c],
            "ck": cache_k[:, c].reshape(2, PAST, D),
            "cv": cache_v[:, c].reshape(2, PAST, D),
            "clf": cache_logf[:, c],
            "sssm": state_ssm[:, c].reshape(2, 2048, 128),
            "sconv": np.ascontiguousarray(state_conv[:, c].reshape(2, 3, 32, 128).transpose(3, 0, 2, 1)),
            "w_in": w_in, "w_br_m": w_br_m, "w_br_a": w_br_a, "w_out": w_out,
            "pbc": pbc, "ppp": pp,
        })
    res = run_bass_kernel_spmd(nc, in_maps, core_ids=list(range(NCORE)))
    R = res.results
    cat = lambda k, ax: np.concatenate([np.asarray(r[k]) for r in R], axis=ax)
    stk = lambda k, ax: np.stack([np.asarray(r[k]) for r in R], axis=ax)
    y_prompt = cat("y_p", 0)
    y_sample = stk("y_s", 0)
    new_k_p = cat("nk_p", 1).reshape(2, B, SEQ, NH_A, 64)
    new_v_p = cat("nv_p", 1).reshape(2, B, SEQ, NH_A, 64)
    new_logf_p = cat("nlf_p", 1)
    new_ssm_p = cat("nssm_p", 1).reshape(2, B, 32, 64, 128)
    ncp = cat("nconv_p", 1)
    new_conv_p = np.ascontiguousarray(ncp.transpose(0, 1, 4, 3, 2)).reshape(2, B, 3, 4096)
    new_k_s = stk("nk_s", 1).reshape(2, NCORE, DEC, NH_A, 64)
    new_v_s = stk("nv_s", 1).reshape(2, NCORE, DEC, NH_A, 64)
    new_logf_s = stk("nlf_s", 1)
    new_ssm_s = stk("nssm_s", 1).reshape(2, NCORE, 32, 64, 128)
    ncs = stk("nconv_s", 1)
    new_conv_s = np.ascontiguousarray(ncs.transpose(0, 1, 4, 3, 2)).reshape(2, NCORE, 3, 4096)
    return (y_prompt.astype(np.float32), y_sample.astype(np.float32), new_k_p, new_v_p, new_logf_p, new_ssm_p, new_conv_p,
            new_k_s, new_v_s, new_logf_s, new_ssm_s, new_conv_s)
```

```python
import numpy as np
import concourse.bass as bass
import concourse.mybir as mybir
from concourse.bass_utils import run_bass_kernel_spmd

F32 = mybir.dt.float32
BF16 = mybir.dt.bfloat16
AF = mybir.ActivationFunctionType
ALU = mybir.AluOpType

ENGS = ("pe", "act", "dve", "pool", "sp")

D = 1024
DIN = 12336
NH_M = 32
NH_A = 16
C_Z, C_X, C_B, C_C, C_DT, C_Q, C_K, C_V, C_F, C_G, C_MM, C_MA = 0, 2048, 4096, 5120, 6144, 6176, 7200, 8224, 9248, 9264, 10288, 11312
DN_ALPHA = (2 * 2) ** 0.25
NPB_L = 2160
NPP_L = 192


class Buf:
    __slots__ = ("name", "w", "r", "dw", "dr", "wsem", "rsem", "wcnt", "rcnt", "psum")

    def __init__(self, name):
        self.name = name
        self.psum = False
        self.w = {}
        self.r = {}
        self.dw = {}
        self.dr = {}
        self.wsem = {}
        self.rsem = {}
        self.wcnt = {}
        self.rcnt = {}


class DBuf:
    def __init__(self, name):
        self.name = name
        self.w = {}


class Tile:
    def __init__(self, t, buf):
        self.t = t
        self.buf = buf

    def __getitem__(self, k):
        return self.t[k]


class Rot:
    def __init__(self, items):
        self.items = items
        self.i = 0

    def next(self):
        x = self.items[self.i % len(self.items)]
        self.i += 1
        return x


class Sched:
    def __init__(self, nc):
        self.nc = nc
        self.ops = {e: [] for e in ENGS}
        self.out_waits = []
        self.nsem = 0
        self.phase = "pro"

    def _sem(self, name):
        self.nsem += 1
        return self.nc.alloc_semaphore(name=f"s{self.nsem}_{name}")

    def sbuf(self, name, shape, dtype):
        return Tile(self.nc.alloc_sbuf_tensor(name, list(shape), dtype), Buf(name))

    def psum(self, name, shape, dtype=F32):
        t = Tile(self.nc.alloc_psum_tensor(name, list(shape), dtype), Buf(name))
        t.buf.psum = True
        return t

    @staticmethod
    def _bufs(xs):
        out = []
        for x in xs:
            if x is None:
                continue
            out.append(x.buf if isinstance(x, Tile) else x)
        return out

    def op(self, eng, fn, reads=(), writes=()):
        reads = self._bufs(reads)
        writes = self._bufs(writes)
        deps = []
        for b in reads:
            for e, i in b.w.items():
                if e == eng and eng == "pe":
                    continue
                deps.append(("e", e, i))
            for sc in b.dw.values():
                deps.append(("d",) + sc)
            if b.psum:
                for e, i in b.r.items():
                    if e != eng:
                        deps.append(("e", e, i))
        for b in writes:
            for e, i in b.w.items():
                if e != eng or eng != "pe":
                    deps.append(("e", e, i))
            for e, i in b.r.items():
                if e != eng or eng != "pe":
                    deps.append(("e", e, i))
            for sc in b.dw.values():
                deps.append(("d",) + sc)
            for sc in b.dr.values():
                deps.append(("d",) + sc)
        idx = len(self.ops[eng])
        self.ops[eng].append({"fn": fn, "deps": deps, "signal": False, "dma": None, "phase": self.phase})
        for b in reads:
            b.r[eng] = idx
        for b in writes:
            b.w[eng] = idx
        return idx

    def dma(self, q, out_ap, in_ap, reads=(), writes=(), final=False, dram_r=None, dram_w=None, serialize=False, **kw):
        reads = self._bufs(reads)
        writes = self._bufs(writes)
        assert len(reads) + len(writes) == 1
        deps = []
        sw = (q == "pool")
        for b in reads:
            for e, i in b.w.items():
                deps.append(("e", e, i))
            for sc in b.dw.values():
                deps.append(("d",) + sc)
        for b in writes:
            for e, i in b.w.items():
                deps.append(("e", e, i))
            for e, i in b.r.items():
                deps.append(("e", e, i))
            for sc in b.dr.values():
                deps.append(("d",) + sc)
            if serialize:
                for sc in b.dw.values():
                    deps.append(("d",) + sc)
        if dram_r is not None:
            for s_, c_ in dram_r.w.values():
                deps.append(("d", s_, c_))
        sem = None
        for b in writes:
            if sw not in b.wsem:
                b.wsem[sw] = self._sem(("ws" if sw else "wh") + b.name)
                b.wcnt[sw] = 0
            b.wcnt[sw] += 16
            sem, cnt = b.wsem[sw], b.wcnt[sw]
            b.dw[id(sem)] = (sem, cnt)
        for b in reads:
            if sw not in b.rsem:
                b.rsem[sw] = self._sem(("rs" if sw else "rh") + b.name)
                b.rcnt[sw] = 0
            b.rcnt[sw] += 16
            sem, cnt = b.rsem[sw], b.rcnt[sw]
            b.dr[id(sem)] = (sem, cnt)
            if final:
                self.out_waits.append(b)
        if dram_w is not None:
            dram_w.w[id(sem)] = (sem, cnt)

        def fn(eng, out_ap=out_ap, in_ap=in_ap, kw=kw):
            return eng.dma_start(out=out_ap, in_=in_ap, **kw)
        self.ops[q].append({"fn": fn, "deps": deps, "signal": False, "dma": sem})

    def emit(self):
        nc = self.nc
        for e in ENGS:
            for o in self.ops[e]:
                for d in o["deps"]:
                    if d[0] == "e":
                        self.ops[d[1]][d[2]]["signal"] = True
        esem = {}
        for e in ENGS:
            c = 0
            for o in self.ops[e]:
                if o["signal"]:
                    assert o["dma"] is None
                    c += 1
                    o["sigval"] = c
            if c:
                esem[e] = self._sem("eng_" + e)
        out_waits = self.out_waits

        def body(e):
            def f(eng):
                known = {}
                for o in self.ops[e]:
                    need = {}
                    for d in o["deps"]:
                        if d[0] == "e":
                            s = esem[d[1]]
                            v = self.ops[d[1]][d[2]]["sigval"]
                        else:
                            s, v = d[1], d[2]
                        k = id(s)
                        if known.get(k, 0) >= v:
                            continue
                        if k not in need or need[k][1] < v:
                            need[k] = (s, v)
                    for k, (s, v) in need.items():
                        eng.wait_ge(s, v)
                        known[k] = v
                    inst = o["fn"](eng)
                    if o["dma"] is not None:
                        inst.then_inc(o["dma"], 16)
                    elif o["signal"]:
                        inst.then_inc(esem[e], 1)
                if e == "pool":
                    seen = set()
                    for b in out_waits:
                        if id(b) in seen:
                            continue
                        seen.add(id(b))
                        for sem_, cnt_ in b.dr.values():
                            eng.wait_ge(sem_, cnt_)
            return f

        with nc.Block() as block:
            block.tensor(body("pe"))
            block.scalar(body("act"))
            block.vector(body("dve"))
            block.gpsimd(body("pool"))
            block.sync(body("sp"))


def build(SEQ, PAST, NB, DEC=16):
    NCHP = SEQ // 128
    NCHS = PAST // 128
    NCH = max(NCHP, NCHS + 1)
    nc = bass.Bass("TRN2", target_bir_lowering=False)

    def din(name, shape, dt=F32):
        return nc.dram_tensor(name, list(shape), dt, kind="ExternalInput").ap()

    def dout(name, shape, dt=F32):
        return nc.dram_tensor(name, list(shape), dt, kind="ExternalOutput").ap()

    def dint(name, shape, dt=BF16):
        return nc.dram_tensor(name, list(shape), dt, kind="Internal").ap()

    xp = din("xp", [NB, SEQ, D])
    xs = din("xs", [DEC, D])
    ck = din("ck", [2, PAST, D])
    cv = din("cv", [2, PAST, D])
    clf = din("clf", [2, PAST, NH_A])
    sssm = din("sssm", [2, 2048, 128])
    sconv = din("sconv", [128, 2, 32, 3])
    w_in = din("w_in", [2, D, DIN])
    w_br_m = din("w_br_m", [2, 2048, D])
    w_br_a = din("w_br_a", [2, D, D])
    w_out = din("w_out", [2, D, D])
    pbc_d = din("pbc", [128, 2 * NPB_L])
    ppp_d = din("ppp", [128, 2 * NPP_L])

    y_p = dout("y_p", [NB, SEQ, D])
    y_s = dout("y_s", [DEC, D])
    nk_p = dout("nk_p", [2, NB, SEQ, D])
    nv_p = dout("nv_p", [2, NB, SEQ, D])
    nlf_p = dout("nlf_p", [2, NB, SEQ, NH_A])
    nssm_p = dout("nssm_p", [2, NB, 2048, 128])
    nconv_p = dout("nconv_p", [2, NB, 128, 32, 3])
    nk_s = dout("nk_s", [2, DEC, D])
    nv_s = dout("nv_s", [2, DEC, D])
    nlf_s = dout("nlf_s", [2, DEC, NH_A])
    nssm_s = dout("nssm_s", [2, 2048, 128])
    nconv_s = dout("nconv_s", [2, 128, 32, 3])

    wb_in = dint("wb_in", [2, D, DIN])
    wb_m = dint("wb_m", [2, 2048, D])
    wb_a = dint("wb_a", [2, D, D])
    wb_o = dint("wb_o", [2, D, D])
    ktd = dint("ktd", [2, NCHP, 128, 8, 128])
    vd = dint("vd", [2, NCHP, 128, D])

    S = Sched(nc)
    import os as _os
    _KSTOP = int(_os.environ.get("KSTOP", "0"))
    DW = DBuf("weights")
    DKT = [DBuf("ktd0"), DBuf("ktd1")]
    DV = [DBuf("vd0"), DBuf("vd1")]

    ident = S.sbuf("ident", [128, 128], F32)
    tri_f = S.sbuf("tri_f", [128, 128], F32)
    gt_f = S.sbuf("gt_f", [128, 128], F32)
    ones_f = S.sbuf("ones_f", [128, 128], F32)
    maskneg = S.sbuf("maskneg", [128, 128], F32)
    tri_b = S.sbuf("tri_b", [128, 128], BF16)
    ones_b = S.sbuf("ones_b", [128, 64], BF16)
    selh = S.sbuf("selh", [80, 16, 128], BF16)
    pbc = S.sbuf("pbc_sb", [128, 2 * NPB_L], F32)
    ppp = S.sbuf("ppp_sb", [128, 2 * NPP_L], F32)
    A_bc = S.sbuf("A_bc", [128, 2, 32], F32)

    xres = [S.sbuf(f"xres{i}", [128, D], F32) for i in range(2)]
    xT = S.sbuf("xT", [128, 8, 128], BF16)
    wbufs = Rot([S.sbuf(f"wbuf{i}", [128, 4096], BF16) for i in range(4)])
    qT = S.sbuf("qT", [128, 8, 128], BF16)
    sgT = S.sbuf("sgT", [128, 8, 128], F32)
    yaT = S.sbuf("yaT", [128, 8, 128], BF16)
    ptr = Rot([S.sbuf(f"pt{i}", [128, 128], BF16) for i in range(6)])
    kvst = Rot([S.sbuf(f"kvst{i}", [128, D], F32) for i in range(2)])
    ktcur = Rot([S.sbuf(f"ktcur{i}", [128, 8, 128], BF16) for i in range(2)])
    vcur = Rot([S.sbuf(f"vcur{i}", [128, D], BF16) for i in range(2)])
    ktl = Rot([S.sbuf(f"ktl{i}", [128, 8, 128], BF16) for i in range(3)])
    vl = Rot([S.sbuf(f"vl{i}", [128, D], BF16) for i in range(3)])
    negc = [S.sbuf(f"negc{l}", [128, NCH, 16], F32) for l in range(2)]
    ccar = [S.sbuf(f"ccar{l}", [128, 16], F32) for l in range(2)]
    cTb = S.sbuf("cTb", [80, 128], BF16)
    cwide = S.sbuf("cwide", [128, 80], F32)
    sm = S.sbuf("sm", [128, 16, 32], F32)
    ST = [S.sbuf(f"ST{l}", [128, 2048], F32) for l in range(2)]
    STb = S.sbuf("STb", [128, 2048], BF16)
    cprev = [S.sbuf(f"cprev{l}", [128, 32, 3], F32) for l in range(2)]
    xpad = Rot([S.sbuf(f"xpad{i}", [128, 132], F32) for i in range(4)])
    cva = Rot([S.sbuf(f"cva{i}", [128, 128], F32) for i in range(3)])
    cvb = Rot([S.sbuf(f"cvb{i}", [128, 128], F32) for i in range(4)])
    xdt = S.sbuf("xdt", [128, 2048], BF16)
    xw = S.sbuf("xw", [128, 2048], BF16)
    xD = S.sbuf("xD", [128, 2048], F32)
    ybuf = S.sbuf("ybuf", [128, 2048], F32)
    BT = S.sbuf("BT", [128, 8, 128], BF16)
    CT = S.sbuf("CT", [128, 8, 128], BF16)
    Btok = S.sbuf("Btok", [128, 8, 128], BF16)
    cbm = S.sbuf("cbm", [128, 8, 128], F32)
    xsB = cbm
    rhsall = Rot([S.sbuf(f"rhsall{i}", [128, 4, 128], F32) for i in range(3)])
    eseg = Rot([S.sbuf(f"eseg{i}", [128, 4, 128], F32) for i in range(3)])
    MTt = Rot([S.sbuf(f"MT{i}", [128, 4, 128], BF16) for i in range(4)])
    t256 = Rot([S.sbuf(f"t256_{i}", [128, 256], F32) for i in range(2)])
    t512 = Rot([S.sbuf(f"t512_{i}", [128, 512], F32) for i in range(2)])
    ymT = S.sbuf("ymT", [128, 16, 128], BF16)
    gaT = S.sbuf("gaT", [128, 8, 128], F32)
    gmT = S.sbuf("gmT", [128, 8, 128], F32)
    uTf = sgT
    uT = S.sbuf("uT", [128, 8, 128], BF16)
    pre = ybuf
    clft = S.sbuf("clft", [128, max(NCHS, 1), 16], F32)
    castb = Buf("castb")

    banks = [S.psum(f"bank{i}", [128, 512], F32) for i in range(8)]
    psA = Rot(banks[0:4])
    ps6 = Rot(banks[0:6])
    psAll = Rot(banks)
    ybk = Rot(banks[4:8])
    oTb = banks[4:6]
    rsb = banks[6:8]

    def PB(l, off, n):
        return pbc.t[:, l * NPB_L + off: l * NPB_L + off + n]

    def PP(l, off, n=1):
        return ppp.t[:, l * NPP_L + off: l * NPP_L + off + n]

    def mm(out, lhsT, rhs, start, stop, reads, writes, **kw):
        S.op("pe", lambda e: e.matmul(out, lhsT=lhsT, rhs=rhs, start=start, stop=stop, **kw), reads=reads, writes=writes)

    def tp(out, in_, n_in_part, reads, writes):
        S.op("pe", lambda e: e.transpose(out=out, in_=in_, identity=ident.t[0:n_in_part, 0:n_in_part]), reads=list(reads) + [ident], writes=writes)

    def act(out, in_, func, reads, writes, **kw):
        S.op("act", lambda e: e.activation(out=out, in_=in_, func=func, **kw), reads=reads, writes=writes)

    def tt(eng, out, in0, in1, op, reads, writes):
        S.op(eng, lambda e: e.tensor_tensor(out=out, in0=in0, in1=in1, op=op), reads=reads, writes=writes)

    def ts(eng, out, in0, s1, s2, op0, op1, reads, writes):
        if op1 is None:
            S.op(eng, lambda e: e.tensor_scalar(out=out, in0=in0, scalar1=s1, scalar2=None, op0=op0), reads=reads, writes=writes)
        else:
            S.op(eng, lambda e: e.tensor_scalar(out=out, in0=in0, scalar1=s1, scalar2=s2, op0=op0, op1=op1), reads=reads, writes=writes)

    def cp(eng, out, in_, reads, writes):
        if eng == "act":
            S.op(eng, lambda e: e.activation(out=out, in_=in_, func=AF.Copy), reads=reads, writes=writes)
        else:
            S.op(eng, lambda e: e.tensor_copy(out=out, in_=in_), reads=reads, writes=writes)

    def ms(eng, ap, val, writes):
        S.op(eng, lambda e: e.memset(ap, val), writes=writes)

    ms("pool", ident.t[:, :], 0.0, [ident])
    S.op("pool", lambda e: e.affine_select(out=ident.t[:, :], in_=ident.t[:, :], pattern=[[-1, 128]], compare_op=ALU.not_equal,
                                           fill=1.0, base=0, channel_multiplier=1), reads=[ident], writes=[ident])
    ms("pool", tri_f.t[:, :], 1.0, [tri_f])
    S.op("pool", lambda e: e.affine_select(out=tri_f.t[:, :], in_=tri_f.t[:, :], pattern=[[1, 128]], compare_op=ALU.is_ge,
                                           fill=0.0, base=0, channel_multiplier=-1), reads=[tri_f], writes=[tri_f])
    ms("pool", gt_f.t[:, :], 1.0, [gt_f])
    S.op("pool", lambda e: e.affine_select(out=gt_f.t[:, :], in_=gt_f.t[:, :], pattern=[[-1, 128]], compare_op=ALU.is_gt,
                                           fill=0.0, base=0, channel_multiplier=1), reads=[gt_f], writes=[gt_f])
    ms("pool", ones_f.t[:, :], 1.0, [ones_f])
    ms("pool", maskneg.t[:, :], 0.0, [maskneg])
    S.op("pool", lambda e: e.affine_select(out=maskneg.t[:, :], in_=maskneg.t[:, :], pattern=[[1, 128]], compare_op=ALU.is_ge,
                                           fill=-30000.0, base=0, channel_multiplier=-1), reads=[maskneg], writes=[maskneg])
    ms("pool", ones_b.t[:, :], 1.0, [ones_b])
    cp("pool", tri_b.t[:, :], tri_f.t[:, :], [tri_f], [tri_b])
    ms("pool", selh.t[:, :, :], 0.0, [selh])
    for p0 in (0, 64):
        S.op("pool", lambda e, p0=p0: e.affine_select(out=selh.t[p0:p0 + 16, :, :], in_=selh.t[p0:p0 + 16, :, :], pattern=[[1, 16], [0, 128]], compare_op=ALU.not_equal,
                                                      fill=1.0, base=0, channel_multiplier=-1), reads=[selh], writes=[selh])
    ms("pool", cwide.t[:, :], 0.0, [cwide])
    S.dma("sp", pbc.t[:, :], pbc_d, writes=[pbc])
    S.dma("sp", ppp.t[:, :], ppp_d, writes=[ppp])
    for l in range(2):
        act(A_bc.t[:, l, :], PB(l, 32, 32), AF.Exp, [pbc], [A_bc])
    ts("dve", A_bc.t[:, :, :], A_bc.t[:, :, :], -1.0, None, ALU.mult, None, [A_bc], [A_bc])

    for l in range(2):
        for r0 in range(0, D, 256):
            S.dma("pool", wb_in[l, r0:r0 + 256, :].rearrange("r (a b) -> r a b", b=1542),
                  w_in[l, r0:r0 + 256, :].rearrange("r (a b) -> r a b", b=1542), writes=[castb], dram_w=DW, serialize=True)
        for r0 in range(0, 2048, 1024):
            S.dma("pool", wb_m[l, r0:r0 + 1024, :], w_br_m[l, r0:r0 + 1024, :], writes=[castb], dram_w=DW, serialize=True)
        S.dma("pool", wb_a[l, :, :], w_br_a[l, :, :], writes=[castb], dram_w=DW, serialize=True)
        S.dma("pool", wb_o[l, :, :], w_out[l, :, :], writes=[castb], dram_w=DW, serialize=True)

    def wload(src2d, nk, c0, ncols):
        wt = wbufs.next()
        view = wt.t[:, 0:nk * ncols].rearrange("p (k n) -> p k n", n=ncols)
        S.dma("sp", view, src2d[:, c0:c0 + ncols].rearrange("(k p) n -> p k n", p=128), writes=[wt], dram_r=DW)
        return wt, view

    class _Stop(Exception):
        pass

    def stage(n):
        S.phase = f"st{n}"
        if _KSTOP == n:
            raise _Stop()

    def tile_layer(l, TS, xr, chunk_idx, past_chunks, out_k, out_v, out_lf):
        Win = wb_in[l]
        for g in range(2):
            bk = psAll.next()
            for c in range(4):
                tp(bk.t[:, c * 128:c * 128 + TS], xr.t[0:TS, (4 * g + c) * 128:(4 * g + c + 1) * 128], TS, [xr], [bk])
            act(xT.t[:, 4 * g:4 * g + 4, 0:TS], bk.t[:, :].rearrange("p (c t) -> p c t", t=128)[:, :, 0:TS], AF.Copy, [bk], [xT])

        def proj_fm(src2d, c0, nchunk, nk, rhsT, evac):
            done = 0
            while done < nchunk:
                n = min(4, nchunk - done)
                wt, wv = wload(src2d, nk, c0 + done * 128, n * 128)
                for jj in range(n):
                    bk = psAll.next()
                    for kc in range(nk):
                        mm(bk.t[:, 0:TS], wv[:, kc, jj * 128:(jj + 1) * 128], rhsT.t[:, kc, 0:TS], kc == 0, kc == nk - 1, [wt, rhsT], [bk])
                    evac(done + jj, bk)
                done += n

        def proj_tm(src2d, c0, ncols, nk, lhsT, evac, wpre=None):
            if wpre is None:
                wt, wv = wload(src2d, nk, c0, ncols)
            else:
                wt, wv = wpre
            bk = psAll.next()
            for kc in range(nk):
                mm(bk.t[0:TS, 0:ncols], lhsT.t[:, kc, 0:TS], wv[:, kc, 0:ncols], kc == 0, kc == nk - 1, [wt, lhsT], [bk])
            evac(bk)

        stage(10)
        proj_fm(Win, C_Q, 8, 8, xT, lambda j, bk: act(qT.t[:, j, 0:TS], bk.t[:, 0:TS], AF.Copy, [bk], [qT], scale=0.125))
        ktc = ktcur.next()
        vc = vcur.next()
        kst_t = kvst.next()
        for blk in range(2):
            proj_tm(Win, C_K + blk * 512, 512, 8, xT, lambda bk, blk=blk: cp("dve", kst_t.t[0:TS, blk * 512:(blk + 1) * 512], bk.t[0:TS, 0:512], [bk], [kst_t]))
        stage(11)
        S.dma("pool", out_k, kst_t.t[0:TS, :], reads=[kst_t], final=True)
        stage(12)
        vst_t = kvst.next()
        for blk in range(2):
            def ev(bk, blk=blk):
                cp("dve", vst_t.t[0:TS, blk * 512:(blk + 1) * 512], bk.t[0:TS, 0:512], [bk], [vst_t])
                act(vc.t[0:TS, blk * 512:(blk + 1) * 512], bk.t[0:TS, 0:512], AF.Copy, [bk], [vc])
            proj_tm(Win, C_V + blk * 512, 512, 8, xT, ev)
        S.dma("pool", out_v, vst_t.t[0:TS, :], reads=[vst_t], final=True)
        for g2 in range(2):
            bk = psAll.next()
            for cc in range(4):
                j = g2 * 4 + cc
                tp(bk.t[:, cc * 128:cc * 128 + TS], kst_t.t[0:TS, j * 128:(j + 1) * 128], TS, [kst_t], [bk])
            act(ktc.t[:, g2 * 4:g2 * 4 + 4, 0:TS], bk.t[:, :].rearrange("p (c t) -> p c t", t=128)[:, :, 0:TS], AF.Copy, [bk], [ktc])
        if past_chunks is not None and chunk_idx is not None and TS == 128 and chunk_idx < NCHP - 1:
            S.dma("pool", ktd[l, chunk_idx], ktc.t[:, :, :], reads=[ktc], dram_w=DKT[l], final=True)
            S.dma("pool", vd[l, chunk_idx], vc.t[:, :], reads=[vc], dram_w=DV[l], final=True)
        stage(13)
        ci = chunk_idx
        LSP, LF, CC = sm.t[:, 0, 0:16], sm.t[:, 1, 0:16], sm.t[:, 2, 0:16]

        def ev_f(bk):
            tt("dve", sm.t[0:TS, 0, 0:16], bk.t[0:TS, 0:16], PB(l, 96, 16)[0:TS, :], ALU.add, [bk, pbc], [sm])
            act(sm.t[0:TS, 0, 0:16], sm.t[0:TS, 0, 0:16], AF.Exp, [sm], [sm], scale=-1.0)
            act(sm.t[0:TS, 0, 0:16], sm.t[0:TS, 0, 0:16], AF.Ln, [sm], [sm], bias=1.0, scale=1.0)
            ts("dve", sm.t[0:TS, 1, 0:16], sm.t[0:TS, 0, 0:16], -1.0, None, ALU.mult, None, [sm], [sm])
        proj_tm(Win, C_F, 16, 8, xT, ev_f)
        proj_fm(Win, C_G, 8, 8, xT, lambda j, bk: act(sgT.t[:, j, 0:TS], bk.t[:, 0:TS], AF.Silu, [bk], [sgT]))
        S.dma("pool", out_lf, sm.t[0:TS, 1, 0:16], reads=[sm], final=True)
        cumsum_chunk(l, TS, sm.t[0:TS, 1, 0:16], sm, ci)
        ts("dve", cwide.t[0:TS, 0:16], negc[l].t[0:TS, ci, :], -1.0, None, ALU.mult, None, [negc[l]], [cwide])
        ts("dve", cwide.t[0:TS, 64:80], negc[l].t[0:TS, ci, :], -1.0, None, ALU.mult, None, [negc[l]], [cwide])
        bk = psAll.next()
        tp(bk.t[0:80, 0:TS], cwide.t[0:TS, 0:80], TS, [cwide], [bk])
        act(cTb.t[0:80, 0:TS], bk.t[0:80, 0:TS], AF.Copy, [bk], [cTb])
        stage(14)
        stage(15)
        chunks = []
        for pc_i, pc in enumerate(past_chunks):
            chunks.append((pc, pc_i, 128, False))
        chunks.append((lambda: (ktc, vc), ci, TS, True))
        for half in range(2):
          ob_h = banks[6]
          rb_h = banks[7]
          for b_ in (ob_h, rb_h):
            ms("dve", b_.t[:, :], 0.0, [b_])
          items = [(cix, j) for cix in range(len(chunks)) for j in range(half * 4, half * 4 + 4)]
          loaded = {}

          def stA(it):
              cix, j = it
              getter, cidx, nk, diag = chunks[cix]
              if cix not in loaded:
                  loaded[cix] = getter()
              kt_t, v_t = loaded[cix]
              sbs = [ps6.next(), ps6.next()]
              for e_ in range(2):
                  hb = 64 * e_
                  mm(sbs[e_].t[0:nk, 0:TS], kt_t.t[hb:hb + 64, j, 0:nk], qT.t[hb:hb + 64, j, 0:TS], True, False, [kt_t, qT], [sbs[e_]])
              for e_ in range(2):
                  hb = 64 * e_
                  h = 2 * j + e_
                  mm(sbs[e_].t[0:nk, 0:TS], selh.t[hb:hb + 16, h, 0:nk], cTb.t[hb:hb + 16, 0:TS], False, True, [selh, cTb], [sbs[e_]])
              pts_ = []
              for e_ in range(2):
                  h = 2 * j + e_
                  sb = sbs[e_]
                  pt = ptr.next()
                  if diag:
                      tm_ = cva.next()
                      S.op("dve", lambda e, tm_=tm_, sb=sb, h=h, cidx=cidx, nk=nk: e.scalar_tensor_tensor(
                          out=tm_.t[0:nk, 0:TS], in0=sb.t[0:nk, 0:TS], scalar=negc[l].t[0:nk, cidx, h:h + 1], in1=maskneg.t[0:nk, 0:TS],
                          op0=ALU.add, op1=ALU.add), reads=[sb, negc[l], maskneg], writes=[tm_])
                      act(pt.t[0:nk, 0:TS], tm_.t[0:nk, 0:TS], AF.Exp, [tm_], [pt])
                  else:
                      act(pt.t[0:nk, 0:TS], sb.t[0:nk, 0:TS], AF.Exp, [sb, negc[l]], [pt], bias=negc[l].t[0:nk, cidx, h:h + 1], scale=1.0)
                  pts_.append(pt)
              return pts_

          def stB(it, pts_):
              cix, j = it
              getter, cidx, nk, diag = chunks[cix]
              kt_t, v_t = loaded[cix]
              ob = ob_h
              rb = rb_h
              col = (j % 4) * 128
              for e_ in range(2):
                  hb = 64 * e_
                  h = 2 * j + e_
                  mm(ob.t[hb:hb + 64, col:col + TS], v_t.t[0:nk, h * 64:(h + 1) * 64], pts_[e_].t[0:nk, 0:TS], False, False, [v_t, pts_[e_]], [ob], skip_group_check=True)
              for e_ in range(2):
                  hb = 64 * e_
                  mm(rb.t[hb:hb + 64, col:col + TS], ones_b.t[0:nk, 0:64], pts_[e_].t[0:nk, 0:TS], False, False, [ones_b, pts_[e_]], [rb], skip_group_check=True)

          LOOK = 2
          pts = {}
          for i in range(len(items) + LOOK):
              if i < len(items):
                  pts[i] = stA(items[i])
              if i - LOOK >= 0:
                  stB(items[i - LOOK], pts.pop(i - LOOK))
          tmp = t512.next()
          tv = tmp.t[:, :].rearrange("p (c t) -> p c t", t=128)[:, :, 0:TS]
          S.op("dve", lambda e, tv=tv, rb_h=rb_h: e.reciprocal(out=tv, in_=rb_h.t[:, :].rearrange("p (c t) -> p c t", t=128)[:, :, 0:TS]), reads=[rb_h], writes=[tmp])
          tt("dve", tv, ob_h.t[:, :].rearrange("p (c t) -> p c t", t=128)[:, :, 0:TS], tv, ALU.mult, [ob_h, tmp], [tmp])
          tt("pool", yaT.t[:, 4 * half:4 * half + 4, 0:TS], tv, sgT.t[:, 4 * half:4 * half + 4, 0:TS], ALU.mult, [tmp, sgT], [yaT])

        stage(16)
        DT, DTA, ACUM, EA, WEND, EEND, TMPD = (sm.t[:, 3, :], sm.t[:, 4, :], sm.t[:, 5, :], sm.t[:, 6, :], sm.t[:, 7, :], sm.t[:, 8, :], sm.t[:, 9, :])

        def ev_dt(bk):
            tt("dve", DT[0:TS], bk.t[0:TS, 0:32], PB(l, 0, 32)[0:TS, :], ALU.add, [bk, pbc], [sm])
            act(DT[0:TS], DT[0:TS], AF.Exp, [sm], [sm])
            act(DT[0:TS], DT[0:TS], AF.Ln, [sm], [sm], bias=1.0, scale=1.0)
            tt("dve", DTA[0:TS], DT[0:TS], A_bc.t[0:TS, l, :], ALU.mult, [sm, A_bc], [sm])
        proj_tm(Win, C_DT, 32, 8, xT, ev_dt)
        cp("act", STb.t[:, :], ST[l].t[:, :], [ST[l]], [STb])

        stage(17)

        pend = []

        def conv_y(c, tb):
            if c < 16:
                act(ybuf.t[:, c * 128:c * 128 + TS], tb.t[:, 0:TS], AF.Silu, [tb], [ybuf])
            elif c < 24:
                g = c - 16
                act(xsB.t[:, g, 0:TS], tb.t[:, 0:TS], AF.Silu, [tb], [xsB])
                cp("pool", BT.t[:, g, 0:TS], xsB.t[:, g, 0:TS], [xsB], [BT])
            else:
                g = c - 24
                act(CT.t[:, g, 0:TS], tb.t[:, 0:TS], AF.Silu, [tb], [CT])

        nxt_xp = [None]

        def ev_conv(c, bk):
            if nxt_xp[0] is None:
                xp_ = xpad.next()
                cp("pool", xp_.t[:, 0:3], cprev[l].t[:, c, :], [cprev[l]], [xp_])
            else:
                xp_ = nxt_xp[0]
            ta = cva.next()
            tb = cvb.next()
            act(xp_.t[:, 3:3 + TS], bk.t[:, 0:TS], AF.Copy, [bk], [xp_])
            act(ta.t[:, 0:TS], bk.t[:, 0:TS], AF.Identity, [bk, ppp], [ta], scale=PP(l, c * 4 + 3), bias=PP(l, 128 + c))
            if c + 1 < 32:
                nxp = xpad.next()
                cp("pool", nxp.t[:, 0:3], cprev[l].t[:, c + 1, :], [cprev[l]], [nxp])
                nxt_xp[0] = nxp
            else:
                nxt_xp[0] = None
            S.op("dve", lambda e: e.scalar_tensor_tensor(out=tb.t[:, 0:TS], in0=xp_.t[:, 0:TS], scalar=PP(l, c * 4 + 0), in1=ta.t[:, 0:TS], op0=ALU.mult, op1=ALU.add), reads=[xp_, ta, ppp], writes=[tb])
            S.op("dve", lambda e: e.scalar_tensor_tensor(out=ta.t[:, 0:TS], in0=xp_.t[:, 1:1 + TS], scalar=PP(l, c * 4 + 1), in1=tb.t[:, 0:TS], op0=ALU.mult, op1=ALU.add), reads=[xp_, tb, ppp], writes=[ta])
            S.op("dve", lambda e: e.scalar_tensor_tensor(out=tb.t[:, 0:TS], in0=xp_.t[:, 2:2 + TS], scalar=PP(l, c * 4 + 2), in1=ta.t[:, 0:TS], op0=ALU.mult, op1=ALU.add), reads=[xp_, ta, ppp], writes=[tb])
            cp("pool", cprev[l].t[:, c, :], xp_.t[:, TS:TS + 3], [xp_], [cprev[l]])
            pend.append((c, tb))
            if len(pend) > 2:
                conv_y(*pend.pop(0))
        proj_fm(Win, C_X, 32, 8, xT, ev_conv)
        while pend:
            conv_y(*pend.pop(0))
        bk1 = psAll.next()
        mm(bk1.t[0:TS, 0:32], tri_f.t[0:TS, 0:TS], DTA[0:TS], True, True, [tri_f, sm], [bk1])
        bk2 = psAll.next()
        mm(bk2.t[0:128, 0:32], ones_f.t[0:TS, 0:128], DTA[0:TS], True, True, [ones_f, sm], [bk2])
        act(ACUM[0:TS], bk1.t[0:TS, 0:32], AF.Copy, [bk1], [sm])
        act(EA[0:TS], bk1.t[0:TS, 0:32], AF.Exp, [bk1], [sm])
        act(EEND, bk2.t[:, 0:32], AF.Exp, [bk2], [sm])
        tt("dve", TMPD[0:TS], bk2.t[0:TS, 0:32], ACUM[0:TS], ALU.subtract, [bk2, sm], [sm])
        act(TMPD[0:TS], TMPD[0:TS], AF.Exp, [sm], [sm])
        tt("dve", WEND[0:TS], TMPD[0:TS], DT[0:TS], ALU.mult, [sm], [sm])
        for q4 in range(4):
            xb = psAll.next()
            for cc in range(4):
                c = q4 * 4 + cc
                tp(xb.t[0:TS, cc * 128:(cc + 1) * 128], ybuf.t[:, c * 128:c * 128 + TS], 128, [ybuf], [xb])
            c0 = q4 * 512
            h0 = c0 // 64
            xv = xb.t[0:TS, :].rearrange("p (h d) -> p h d", d=64)
            tt("dve", xdt.t[0:TS, c0:c0 + 512].rearrange("p (h d) -> p h d", d=64), xv, DT[0:TS, h0:h0 + 8].unsqueeze(2).broadcast_to([TS, 8, 64]), ALU.mult, [xb, sm], [xdt])
            tt("dve", xw.t[0:TS, c0:c0 + 512].rearrange("p (h d) -> p h d", d=64), xv, WEND[0:TS, h0:h0 + 8].unsqueeze(2).broadcast_to([TS, 8, 64]), ALU.mult, [xb, sm], [xw])
            tt("dve", xD.t[0:TS, c0:c0 + 512].rearrange("p (h d) -> p h d", d=64), xv, PB(l, 64, 32)[0:TS, h0:h0 + 8].unsqueeze(2).broadcast_to([TS, 8, 64]), ALU.mult, [xb, pbc], [xD])
        for q4 in range(2):
            xb = psAll.next()
            for cc in range(4):
                g = q4 * 4 + cc
                tp(xb.t[0:TS, cc * 128:(cc + 1) * 128], xsB.t[:, g, 0:TS], 128, [xsB], [xb])
            act(Btok.t[0:TS, q4 * 4:q4 * 4 + 4, :], xb.t[0:TS, :].rearrange("p (g n) -> p g n", n=128), AF.Copy, [xb], [Btok])

        stage(18)
        for half in range(2):
            bk = psAll.next()
            for gg in range(4):
                g = half * 4 + gg
                mm(bk.t[0:TS, gg * 128:gg * 128 + TS], BT.t[:, g, 0:TS], CT.t[:, g, 0:TS], True, True, [BT, CT], [bk])
            tt("dve", cbm.t[0:TS, half * 4:half * 4 + 4, 0:TS], bk.t[0:TS, :].rearrange("p (g t) -> p g t", t=128)[:, :, 0:TS],
               tri_f.t[0:TS, 0:TS].unsqueeze(1).broadcast_to([TS, 4, TS]), ALU.mult, [bk, tri_f], [cbm])
        def s1(g):
            ra = rhsall.next()
            tt("pool", ra.t[0:TS, :, 0:TS], DTA[0:TS, 4 * g:4 * g + 4].unsqueeze(2).broadcast_to([TS, 4, TS]),
               tri_f.t[0:TS, 0:TS].unsqueeze(1).broadcast_to([TS, 4, TS]), ALU.mult, [sm, tri_f], [ra])
            bk = psA.next()
            if TS == 128:
                mm(bk.t[0:TS, 0:512], gt_f.t[0:TS, 0:TS], ra.t[0:TS, :, :].rearrange("p r t -> p (r t)"), True, True, [gt_f, ra], [bk])
            else:
                for r in range(4):
                    mm(bk.t[0:TS, r * 128:r * 128 + TS], gt_f.t[0:TS, 0:TS], ra.t[0:TS, r, 0:TS], True, True, [gt_f, ra], [bk])
            es = eseg.next()
            act(es.t[0:TS, :, 0:TS], bk.t[0:TS, :].rearrange("p (r t) -> p r t", t=128)[:, :, 0:TS], AF.Exp, [bk], [es])
            mt = MTt.next()
            tt("dve", mt.t[0:TS, :, 0:TS], es.t[0:TS, :, 0:TS], cbm.t[0:TS, g, 0:TS].unsqueeze(1).broadcast_to([TS, 4, TS]), ALU.mult, [es, cbm], [mt])
            return mt

        def s2(g, mt):
            yb = ybk.next()
            for r in range(4):
                h = 4 * g + r
                mm(yb.t[0:TS, r * 64:(r + 1) * 64], mt.t[0:TS, r, 0:TS], xdt.t[0:TS, h * 64:(h + 1) * 64], True, True, [mt, xdt], [yb])
            mm(yb.t[0:TS, 256:512], CT.t[:, g, 0:TS], STb.t[:, 256 * g:256 * g + 256], True, True, [CT, STb], [yb])
            t1 = t256.next()
            tt("dve", t1.t[0:TS, :].rearrange("p (r d) -> p r d", d=64), yb.t[0:TS, 256:512].rearrange("p (r d) -> p r d", d=64),
               EA[0:TS, 4 * g:4 * g + 4].unsqueeze(2).broadcast_to([TS, 4, 64]), ALU.mult, [yb, sm], [t1])
            tt("dve", t1.t[0:TS, :], yb.t[0:TS, 0:256], t1.t[0:TS, :], ALU.add, [yb, t1], [t1])
            tt("dve", ybuf.t[0:TS, 256 * g:256 * g + 256], t1.t[0:TS, :], xD.t[0:TS, 256 * g:256 * g + 256], ALU.add, [t1, xD], [ybuf])

        mts = {}
        for g in range(8 + 2):
            if g < 8:
                mts[g] = s1(g)
            if g - 2 >= 0:
                s2(g - 2, mts.pop(g - 2))
        stage(19)
        for gp in range(4):
            bk = psAll.next()
            for gg in range(2):
                g = gp * 2 + gg
                mm(bk.t[:, gg * 256:(gg + 1) * 256], Btok.t[0:TS, g, :], xw.t[0:TS, 256 * g:256 * g + 256], True, True, [Btok, xw], [bk])
            sv = ST[l].t[:, gp * 512:(gp + 1) * 512]
            tt("pool", sv.rearrange("p (h d) -> p h d", d=64), sv.rearrange("p (h d) -> p h d", d=64),
               EEND[:, gp * 8:gp * 8 + 8].unsqueeze(2).broadcast_to([128, 8, 64]), ALU.mult, [ST[l], sm], [ST[l]])
            tt("dve", sv, sv, bk.t[:, :], ALU.add, [ST[l], bk], [ST[l]])
        stage(20)
        for blk in range(4):
            def ev_z(bk, blk=blk):
                tz = t512.next()
                act(tz.t[0:TS, :], bk.t[0:TS, :], AF.Silu, [bk], [tz])
                tt("dve", ybuf.t[0:TS, blk * 512:(blk + 1) * 512], ybuf.t[0:TS, blk * 512:(blk + 1) * 512], tz.t[0:TS, :], ALU.mult, [ybuf, tz], [ybuf])
            proj_tm(Win, C_Z + blk * 512, 512, 8, xT, ev_z)
        SS, RST = sm.t[:, 10, 0:8], sm.t[:, 11, 0:8]
        tt("pool", xD.t[0:TS, :], ybuf.t[0:TS, :], ybuf.t[0:TS, :], ALU.mult, [ybuf], [xD])
        S.op("dve", lambda e: e.tensor_reduce(out=SS[0:TS], in_=xD.t[0:TS, :].rearrange("p (g c) -> p g c", c=256), axis=mybir.AxisListType.X, op=ALU.add), reads=[xD], writes=[sm])
        act(RST[0:TS], SS[0:TS], AF.Sqrt, [sm], [sm], bias=1e-5, scale=1.0 / 256.0)
        S.op("dve", lambda e: e.reciprocal(out=RST[0:TS], in_=RST[0:TS]), reads=[sm], writes=[sm])
        tt("dve", ybuf.t[0:TS, :].rearrange("p (g c) -> p g c", c=256), ybuf.t[0:TS, :].rearrange("p (g c) -> p g c", c=256),
           RST[0:TS].unsqueeze(2).broadcast_to([TS, 8, 256]), ALU.mult, [ybuf, sm], [ybuf])
        proj_fm(Win, C_MA, 8, 8, xT, lambda j, bk: act(gaT.t[:, j, 0:TS], bk.t[:, 0:TS], AF.Sigmoid, [bk, ppp], [gaT], bias=PP(l, 176 + 8 + j), scale=1.0))
        proj_fm(Win, C_MM, 8, 8, xT, lambda j, bk: act(gmT.t[:, j, 0:TS], bk.t[:, 0:TS], AF.Sigmoid, [bk, ppp], [gmT], bias=PP(l, 176 + j), scale=1.0))
        proj_fm(wb_a[l], 0, 8, 8, yaT, lambda j, bk: tt("dve", uTf.t[:, j, 0:TS], bk.t[:, 0:TS], gaT.t[:, j, 0:TS], ALU.mult, [bk, gaT], [uTf]))
        for q4 in range(4):
            bk = psAll.next()
            for cc in range(4):
                c = q4 * 4 + cc
                tp(bk.t[:, cc * 128:cc * 128 + TS], ybuf.t[0:TS, c * 128:(c + 1) * 128], TS, [ybuf], [bk])
            tt("dve", ymT.t[:, q4 * 4:q4 * 4 + 4, 0:TS], bk.t[:, :].rearrange("p (c t) -> p c t", t=128)[:, :, 0:TS],
               PP(l, 160 + q4 * 4, 4).unsqueeze(2).broadcast_to([128, 4, TS]), ALU.mult, [bk, ppp], [ymT])

        stage(21)

        def ev_m(j, bk):
            tmp = cva.next()
            tt("dve", tmp.t[:, 0:TS], bk.t[:, 0:TS], gmT.t[:, j, 0:TS], ALU.mult, [bk, gmT], [tmp])
            tt("pool", uT.t[:, j, 0:TS], tmp.t[:, 0:TS], uTf.t[:, j, 0:TS], ALU.add, [tmp, uTf], [uT])
        for blk in range(4):
            wt, wv = wload(wb_m[l], 16, blk * 256, 256)
            for jj in range(2):
                j = blk * 2 + jj
                bk = psAll.next()
                for kc in range(16):
                    mm(bk.t[:, 0:TS], wv[:, kc, jj * 128:(jj + 1) * 128], ymT.t[:, kc, 0:TS], kc == 0, kc == 15, [wt, ymT], [bk])
                ev_m(j, bk)
        for blk in range(2):
            def ev_o(bk, blk=blk):
                S.op("dve", lambda e: e.scalar_tensor_tensor(out=pre.t[0:TS, blk * 512:(blk + 1) * 512], in0=xr.t[0:TS, blk * 512:(blk + 1) * 512], scalar=float(DN_ALPHA),
                                                             in1=bk.t[0:TS, 0:512], op0=ALU.mult, op1=ALU.add), reads=[xr, bk], writes=[pre])
            proj_tm(wb_o[l], blk * 512, 512, 8, uT, ev_o)
        STATS, MV, RS2 = sm.t[:, 12, 0:12], sm.t[:, 13, 0:2], sm.t[:, 13, 2:3]
        S.op("dve", lambda e: e.bn_stats(out=STATS[0:TS, 0:6], in_=pre.t[0:TS, 0:512]), reads=[pre], writes=[sm])
        S.op("dve", lambda e: e.bn_stats(out=STATS[0:TS, 6:12], in_=pre.t[0:TS, 512:1024]), reads=[pre], writes=[sm])
        S.op("dve", lambda e: e.bn_aggr(out=MV[0:TS], in_=STATS[0:TS]), reads=[sm], writes=[sm])
        act(RS2[0:TS], MV[0:TS, 1:2], AF.Sqrt, [sm], [sm], bias=1e-5, scale=1.0)
        S.op("dve", lambda e: e.reciprocal(out=RS2[0:TS], in_=RS2[0:TS]), reads=[sm], writes=[sm])
        ts("dve", pre.t[0:TS, 0:D], pre.t[0:TS, 0:D], MV[0:TS, 0:1], RS2[0:TS], ALU.subtract, ALU.mult, [pre, sm], [pre])
        tt("pool", pre.t[0:TS, 0:D], pre.t[0:TS, 0:D], PB(l, 112, 1024)[0:TS, :], ALU.mult, [pre, pbc], [pre])
        tt("dve", xr.t[0:TS, :], pre.t[0:TS, 0:D], PB(l, 1136, 1024)[0:TS, :], ALU.add, [pre, pbc], [xr])

    def cumsum_chunk(l, TS, lf_ap, lf_tile, ci):
        bk1 = psAll.next()
        mm(bk1.t[0:TS, 0:16], tri_f.t[0:TS, 0:TS], lf_ap, True, True, [tri_f, lf_tile], [bk1])
        bk2 = psAll.next()
        mm(bk2.t[0:128, 0:16], ones_f.t[0:TS, 0:128], lf_ap, True, True, [ones_f, lf_tile], [bk2])
        tt("dve", negc[l].t[0:TS, ci, :], bk1.t[0:TS, 0:16], ccar[l].t[0:TS, :], ALU.add, [bk1, ccar[l]], [negc[l]])
        ts("dve", negc[l].t[0:TS, ci, :], negc[l].t[0:TS, ci, :], -1.0, None, ALU.mult, None, [negc[l]], [negc[l]])
        tt("dve", ccar[l].t[:, :], ccar[l].t[:, :], bk2.t[:, 0:16], ALU.add, [ccar[l], bk2], [ccar[l]])

    def flush_state(l, TS_unused, out_ssm, out_conv):
        for q4 in range(4):
            bk = psAll.next()
            for cc in range(4):
                c = q4 * 4 + cc
                tp(bk.t[:, cc * 128:(cc + 1) * 128], ST[l].t[:, c * 128:(c + 1) * 128], 128, [ST[l]], [bk])
            act(xD.t[:, q4 * 512:(q4 + 1) * 512].rearrange("p (c n) -> p c n", n=128), bk.t[:, :].rearrange("p (c n) -> p c n", n=128), AF.Copy, [bk], [xD])
        S.dma("pool", out_ssm.rearrange("(c p) n -> p c n", p=128), xD.t[:, :].rearrange("p (c n) -> p c n", n=128), reads=[xD], final=True)
        S.dma("pool", out_conv, cprev[l].t[:, :, :], reads=[cprev[l]], final=True)

    def _main_body():
        xi = 0
        for b in range(NB):
            for l in range(2):
                ms("pool", ST[l].t[:, :], 0.0, [ST[l]])
                ms("pool", cprev[l].t[:, :, :], 0.0, [cprev[l]])
                ms("pool", ccar[l].t[:, :], 0.0, [ccar[l]])
            for ci in range(NCHP):
                xr = xres[xi % 2]
                xi += 1
                S.dma("sp", xr.t[:, :], xp[b, ci * 128:(ci + 1) * 128, :], writes=[xr])
                for l in range(2):
                    def mk(c, l=l):
                        def g():
                            kt_t = ktl.next()
                            v_t = vl.next()
                            S.dma("sp", kt_t.t[:, :, :], ktd[l, c], writes=[kt_t], dram_r=DKT[l])
                            S.dma("sp", v_t.t[:, :], vd[l, c], writes=[v_t], dram_r=DV[l])
                            return kt_t, v_t
                        return g
                    tile_layer(l, 128, xr, ci, [mk(c) for c in range(ci)],
                               nk_p[l, b, ci * 128:(ci + 1) * 128, :], nv_p[l, b, ci * 128:(ci + 1) * 128, :], nlf_p[l, b, ci * 128:(ci + 1) * 128, :])
                    stage(30 + l)
                S.dma("pool", y_p[b, ci * 128:(ci + 1) * 128, :], xr.t[:, :], reads=[xr], final=True)
                stage(2)
            for l in range(2):
                flush_state(l, 128, nssm_p[l, b], nconv_p[l, b])
        stage(3)
        xr = xres[xi % 2]
        S.dma("sp", xr.t[0:DEC, :], xs, writes=[xr])
        S.dma("sp", cprev[0].t[:, :, :], sconv[:, 0], writes=[cprev[0]])
        S.dma("sp", cprev[1].t[:, :, :], sconv[:, 1], writes=[cprev[1]])
        for l in range(2):
            S.dma("sp", xD.t[:, :].rearrange("p (c n) -> p c n", n=128), sssm[l].rearrange("(c p) n -> p c n", p=128), writes=[xD])
            for q4 in range(4):
                bk = psAll.next()
                for cc in range(4):
                    c = q4 * 4 + cc
                    tp(bk.t[:, cc * 128:(cc + 1) * 128], xD.t[:, c * 128:(c + 1) * 128], 128, [xD], [bk])
                act(ST[l].t[:, q4 * 512:(q4 + 1) * 512], bk.t[:, :], AF.Copy, [bk], [ST[l]])
            ms("pool", ccar[l].t[:, :], 0.0, [ccar[l]])
            S.dma("sp", clft.t[:, :, :], clf[l].rearrange("(c p) h -> p c h", p=128), writes=[clft])
            for c in range(NCHS):
                cumsum_chunk(l, 128, clft.t[:, c, :], clft, c)

            def mk(c, l=l):
                def g():
                    ks = kvst.next()
                    S.dma("sp", ks.t[:, :], ck[l, c * 128:(c + 1) * 128, :], writes=[ks])
                    kt_t = ktl.next()
                    for g2 in range(2):
                        bk = psA.next()
                        for cc in range(4):
                            tp(bk.t[:, cc * 128:(cc + 1) * 128], ks.t[:, (g2 * 4 + cc) * 128:(g2 * 4 + cc + 1) * 128], 128, [ks], [bk])
                        act(kt_t.t[:, g2 * 4:g2 * 4 + 4, :], bk.t[:, :].rearrange("p (c t) -> p c t", t=128), AF.Copy, [bk], [kt_t])
                    v_t = vl.next()
                    S.dma("pool", v_t.t[:, :], cv[l, c * 128:(c + 1) * 128, :], writes=[v_t])
                    return kt_t, v_t
                return g
            tile_layer(l, DEC, xr, NCHS, [mk(c) for c in range(NCHS)], nk_s[l], nv_s[l], nlf_s[l])
            flush_state(l, DEC, nssm_s[l], nconv_s[l])
        S.dma("pool", y_s, xr.t[0:DEC, :], reads=[xr], final=True)

    try:
        stage(1)
        _main_body()
    except _Stop:
        pass
    S.emit()
    if _os.environ.get("KDUMP"):
        import json as _json
        _json.dump({e: [o.get("phase", "dma") for o in S.ops[e]] for e in ENGS}, open(_os.environ["KDUMP"], "w"))
    return nc


_CACHE = {}


def _host_params(conv_w, conv_b, dt_bias, a_log, d_skip, mnorm_w, b_f, b_merge, ln_g, ln_b):
    pb = np.zeros((2 * NPB_L,), np.float32)
    pp = np.zeros((128, 2 * NPP_L), np.float32)
    for l in range(2):
        o = l * NPB_L
        pb[o:o + 32] = dt_bias[l]
        pb[o + 32:o + 64] = a_log[l]
        pb[o + 64:o + 96] = d_skip[l]
        pb[o + 96:o + 112] = b_f[l]
        pb[o + 112:o + 1136] = ln_g[l]
        pb[o + 1136:o + 2160] = ln_b[l]
        o = l * NPP_L
        pp[:, o:o + 128] = conv_w[l].reshape(4, 32, 128).transpose(2, 1, 0).reshape(128, 128)
        pp[:, o + 128:o + 160] = conv_b[l].reshape(32, 128).T
        pp[:, o + 160:o + 176] = mnorm_w[l].reshape(16, 128).T
        pp[:, o + 176:o + 192] = b_merge[l].reshape(16, 128).T
    pbc = np.ascontiguousarray(np.broadcast_to(pb[None, :], (128, 2 * NPB_L)))
    return pbc, pp


def kernel(x_prompt, x_sample, cache_k, cache_v, cache_logf, state_ssm, state_conv, w_in, conv_w,
           conv_b, dt_bias, a_log, d_skip, mnorm_w, b_f, b_merge, w_br_m, w_br_a, w_out, ln_g, ln_b):
    f = lambda a: np.ascontiguousarray(np.asarray(a, dtype=np.float32))
    x_prompt, x_sample, cache_k, cache_v, cache_logf, state_ssm, state_conv = map(f, (x_prompt, x_sample, cache_k, cache_v, cache_logf, state_ssm, state_conv))
    w_in, w_br_m, w_br_a, w_out = map(f, (w_in, w_br_m, w_br_a, w_out))
    B, SEQ, _ = x_prompt.shape
    NCORE, DEC, _ = x_sample.shape
    PAST = cache_k.shape[2]
    NB = B // NCORE
    key = (SEQ, PAST, NB, DEC)
    if key not in _CACHE:
        _CACHE[key] = build(SEQ, PAST, NB, DEC)
    nc = _CACHE[key]
    pbc, pp = _host_params(*map(f, (conv_w, conv_b, dt_bias, a_log, d_skip, mnorm_w, b_f, b_merge, ln_g, ln_b)))
    in_maps = []
    for c in range(NCORE):
        in_maps.append({
            "xp": x_prompt[c * NB:(c + 1) * NB],
            "xs": x_sample[c],
            "ck": cache_k[:, c].reshape(2, PAST, D),
            "cv": cache_v[:, c].reshape(2, PAST, D),
            "clf": cache_logf[:, c],
            "sssm": state_ssm[:, c].reshape(2, 2048, 128),
            "sconv": np.ascontiguousarray(state_conv[:, c].reshape(2, 3, 32, 128).transpose(3, 0, 2, 1)),
            "w_in": w_in, "w_br_m": w_br_m, "w_br_a": w_br_a, "w_out": w_out,
            "pbc": pbc, "ppp": pp,
        })
    res = run_bass_kernel_spmd(nc, in_maps, core_ids=list(range(NCORE)))
    R = res.results
    cat = lambda k, ax: np.concatenate([np.asarray(r[k]) for r in R], axis=ax)
    stk = lambda k, ax: np.stack([np.asarray(r[k]) for r in R], axis=ax)
    y_prompt = cat("y_p", 0)
    y_sample = stk("y_s", 0)
    new_k_p = cat("nk_p", 1).reshape(2, B, SEQ, NH_A, 64)
    new_v_p = cat("nv_p", 1).reshape(2, B, SEQ, NH_A, 64)
    new_logf_p = cat("nlf_p", 1)
    new_ssm_p = cat("nssm_p", 1).reshape(2, B, 32, 64, 128)
    ncp = cat("nconv_p", 1)
    new_conv_p = np.ascontiguousarray(ncp.transpose(0, 1, 4, 3, 2)).reshape(2, B, 3, 4096)
    new_k_s = stk("nk_s", 1).reshape(2, NCORE, DEC, NH_A, 64)
    new_v_s = stk("nv_s", 1).reshape(2, NCORE, DEC, NH_A, 64)
    new_logf_s = stk("nlf_s", 1)
    new_ssm_s = stk("nssm_s", 1).reshape(2, NCORE, 32, 64, 128)
    ncs = stk("nconv_s", 1)
    new_conv_s = np.ascontiguousarray(ncs.transpose(0, 1, 4, 3, 2)).reshape(2, NCORE, 3, 4096)
    return (y_prompt.astype(np.float32), y_sample.astype(np.float32), new_k_p, new_v_p, new_logf_p, new_ssm_p, new_conv_p,
            new_k_s, new_v_s, new_logf_s, new_ssm_s, new_conv_s)
```
